# Optimizing a Trainium2 kernel written in Bass

```python
import jax
import jax.numpy as jnp
from jax import lax
import numpy as np

D_MODEL = 1024
BATCH = 32
SEQ = 2048
DEPTH = 2

GM_W = 512
GM_GROUPS = 4
GM_CHUNK = 128
ML_H = 4
ML_DH = 128
ML_W = ML_H * ML_DH
ML_CHUNK = 64
CONV_K = 4
NSA_H = 8
NSA_G = 2
NSA_R = NSA_H // NSA_G
NSA_DH = 64
NSA_W = NSA_H * NSA_DH
NSA_KV = NSA_G * NSA_DH
NSA_NB = 3
CMP_BLOCK = 32
CMP_STRIDE = 16
SEL_BLOCK = 64
TOP_N = 8
WINDOW = 512
NSA_QC = 32
N_BRANCH = 3
D_FF = 4 * D_MODEL
EPS = 1e-6
NEG = -1e30
BIG = 1e4
F32 = jnp.float32
SPLIT_SIZES = (GM_W, GM_W, ML_W, ML_W, ML_W, ML_W, ML_H, ML_H, NSA_W, NSA_KV, NSA_KV, NSA_KV, NSA_KV, NSA_KV, NSA_KV, NSA_H * NSA_NB, D_MODEL, D_MODEL, D_MODEL)
P_TOTAL = sum(SPLIT_SIZES)

kernel_name = 'hybrid_gmlp_mlstm_nsa_block'


def rms_norm(x, g):
    x32 = x.astype(F32)
    y = x32 * lax.rsqrt(jnp.mean(x32 * x32, axis=-1, keepdims=True) + EPS)
    return (y * g.astype(F32)).astype(x.dtype)


def layer_norm(x, g, b):
    x32 = x.astype(F32)
    mu = jnp.mean(x32, axis=-1, keepdims=True)
    var = jnp.mean(jnp.square(x32 - mu), axis=-1, keepdims=True)
    return ((x32 - mu) * lax.rsqrt(var + EPS) * g.astype(F32) + b.astype(F32)).astype(x.dtype)


def causal_conv(x, w, b):
    S = x.shape[1]
    xp = jnp.pad(x, ((0, 0), (CONV_K - 1, 0), (0, 0)))
    y = b
    for j in range(CONV_K):
        y = y + xp[:, j:j + S] * w[j]
    return y


def gmlp_mixer(u_pre, v_pre, ln_g, ln_b, ws, bs):
    B, S, _ = u_pre.shape
    u = jax.nn.gelu(u_pre)
    v = layer_norm(jax.nn.gelu(v_pre), ln_g, ln_b)
    dg = GM_W // GM_GROUPS
    v = v.reshape(B, S // GM_CHUNK, GM_CHUNK, GM_GROUPS, dg)
    w = ws * jnp.tril(jnp.ones((GM_CHUNK, GM_CHUNK), ws.dtype))
    mixed = jnp.einsum('gts,bcsgd->bctgd', w, v) + bs.T[:, :, None]
    return u * mixed.reshape(B, S, GM_W)


def mlstm_chunk_step(carry, xs):
    c_mat, n_vec, m_prev = carry
    q, k, v, ig, lf = xs
    L = q.shape[2]
    tril = jnp.tril(jnp.ones((L, L), dtype=bool))
    b = jnp.cumsum(lf, axis=-1)
    a = b + m_prev[..., None]
    d = jnp.where(tril, b[..., :, None] - b[..., None, :] + ig[..., None, :], -jnp.inf)
    m = jnp.maximum(a, jnp.max(d, axis=-1))
    w_inter = jnp.exp(a - m)
    s = jnp.einsum('bhtd,bhsd->bhts', q, k) * jnp.exp(d - m[..., None])
    num = jnp.einsum('bhts,bhse->bhte', s, v) + w_inter[..., None] * jnp.einsum('bhed,bhtd->bhte', c_mat, q)
    den = jnp.sum(s, axis=-1) + w_inter * jnp.einsum('bhd,bhtd->bht', n_vec, q)
    h = num / jnp.maximum(jnp.abs(den), jnp.exp(-m))[..., None]
    m_last = m[..., -1]
    w_prev = jnp.exp(a[..., -1] - m_last)
    w_s = jnp.exp(b[..., -1:] - b + ig - m_last[..., None])
    c_new = w_prev[..., None, None] * c_mat + jnp.einsum('bhs,bhse,bhsd->bhed', w_s, v, k)
    n_new = w_prev[..., None] * n_vec + jnp.einsum('bhs,bhsd->bhd', w_s, k)
    return (c_new, n_new, m_last), h


def mlstm_mixer(q, k, v, o_pre, i_pre, f_pre, conv_w, conv_b, gate_b, norm_g):
    B, S, _ = v.shape
    dtype = v.dtype
    qk = jax.nn.silu(causal_conv(jnp.concatenate([q, k], axis=-1), conv_w, conv_b))
    q, k = jnp.split(qk, 2, axis=-1)
    nc = S // ML_CHUNK

    def heads(t):
        return t.astype(F32).reshape(B, nc, ML_CHUNK, ML_H, ML_DH).transpose(1, 0, 3, 2, 4)

    def gate_heads(t):
        return t.reshape(B, nc, ML_CHUNK, ML_H).transpose(1, 0, 3, 2)

    qh = heads(q)
    kh = heads(k) * (ML_DH ** -0.5)
    vh = heads(v)
    ig = gate_heads((i_pre + gate_b[:ML_H]).astype(F32))
    lf = gate_heads(jax.nn.log_sigmoid((f_pre + gate_b[ML_H:]).astype(F32)))
    init = (jnp.zeros((B, ML_H, ML_DH, ML_DH), F32), jnp.zeros((B, ML_H, ML_DH), F32), jnp.zeros((B, ML_H), F32))
    _, h = lax.scan(mlstm_chunk_step, init, (qh, kh, vh, ig, lf))
    h = h.transpose(1, 0, 3, 2, 4).reshape(B, S, ML_H, ML_DH)
    mu = jnp.mean(h, axis=-1, keepdims=True)
    var = jnp.mean(jnp.square(h - mu), axis=-1, keepdims=True)
    hn = (h - mu) * lax.rsqrt(var + EPS) * norm_g.astype(F32).reshape(ML_H, ML_DH)
    return (jax.nn.sigmoid(o_pre.astype(F32)) * hn.reshape(B, S, ML_W)).astype(dtype)


def compress_blocks(kv, pe, w1, w2):
    B, S, G, dh = kv.shape
    n_cmp = (S - CMP_BLOCK) // CMP_STRIDE + 1
    idx = jnp.arange(n_cmp)[:, None] * CMP_STRIDE + jnp.arange(CMP_BLOCK)[None, :]
    blocks = kv[:, idx] + pe[:, None, :]
    blocks = blocks.transpose(0, 3, 1, 2, 4).reshape(B, G, n_cmp, CMP_BLOCK * dh)
    return jax.nn.gelu(blocks @ w1) @ w2


def masked_softmax(s, valid, axis=-1):
    p = jax.nn.softmax(jnp.where(valid, s, NEG), axis=axis)
    return jnp.where(valid, p, 0.0)


def nsa_mixer(q, kc, vc, ks, vs, kw, vw, gates, pe_k, pe_v, phi_k1, phi_k2, phi_v1, phi_v2):
    B, S, _ = q.shape
    dtype = q.dtype
    qh = q.reshape(B, S, NSA_G, NSA_R, NSA_DH).transpose(0, 2, 3, 1, 4)

    def kv_heads(t):
        return t.reshape(B, S, NSA_G, NSA_DH)

    k_cmp = compress_blocks(kv_heads(kc), pe_k, phi_k1, phi_k2)
    v_cmp = compress_blocks(kv_heads(vc), pe_v, phi_v1, phi_v2).astype(F32)
    n_cmp = k_cmp.shape[2]
    cmp_start = jnp.arange(n_cmp) * CMP_STRIDE
    cmp_end = cmp_start + CMP_BLOCK - 1
    cmp_center = cmp_start.astype(F32) + (CMP_BLOCK - 1) * 0.5
    n_sel = S // SEL_BLOCK
    top_n = min(TOP_N, n_sel)
    sel = jnp.arange(n_sel)
    overlap = ((cmp_start[:, None] <= sel[None, :] * SEL_BLOCK + SEL_BLOCK - 1) & (cmp_end[:, None] >= sel[None, :] * SEL_BLOCK)).astype(F32)
    k_sel = kv_heads(ks).transpose(0, 2, 1, 3).reshape(B, NSA_G, n_sel, SEL_BLOCK, NSA_DH)
    v_sel = kv_heads(vs).transpose(0, 2, 1, 3).reshape(B, NSA_G, n_sel, SEL_BLOCK, NSA_DH)
    pad = ((0, 0), (0, 0), (WINDOW, 0), (0, 0))
    k_win = jnp.pad(kv_heads(kw).transpose(0, 2, 1, 3), pad)
    v_win = jnp.pad(kv_heads(vw).transpose(0, 2, 1, 3), pad)
    g = jax.nn.sigmoid(gates.astype(F32)).reshape(B, S, NSA_G, NSA_R, NSA_NB).transpose(0, 2, 3, 1, 4)
    slopes = (2.0 ** (-8.0 * (jnp.arange(NSA_H, dtype=F32) + 1.0) / NSA_H)).reshape(NSA_G, NSA_R)
    scale = NSA_DH ** -0.5
    bi = jnp.arange(B)[:, None, None, None]
    gi = jnp.arange(NSA_G)[None, :, None, None]

    def query_chunk(qi):
        t0 = qi * NSA_QC
        qb = lax.dynamic_slice_in_dim(qh, t0, NSA_QC, axis=3)
        t = t0 + jnp.arange(NSA_QC)
        tf = t.astype(F32)
        sc = jnp.einsum('bgrqd,bgcd->bgrqc', qb, k_cmp).astype(F32) * scale - slopes[:, :, None, None] * (tf[:, None] - cmp_center[None, :])
        p_c = masked_softmax(sc, cmp_end[None, :] <= t[:, None])
        o_cmp = jnp.einsum('bgrqc,bgcd->bgrqd', p_c, v_cmp)
        imp = jnp.einsum('bgrqc,cj->bgqj', p_c, overlap)
        jt = (t // SEL_BLOCK)[:, None]
        forced = (sel == 0) | (sel == jt) | (sel == jt - 1)
        imp = jnp.where(sel > jt, NEG, jnp.where(forced, BIG, imp))
        _, idx = lax.top_k(imp, top_n)
        kg = k_sel[bi, gi, idx]
        vg = v_sel[bi, gi, idx].astype(F32)
        s_pos = idx[..., None] * SEL_BLOCK + jnp.arange(SEL_BLOCK)
        dist_s = (tf[:, None, None] - s_pos.astype(F32))[:, :, None]
        ss = jnp.einsum('bgrqd,bgqnld->bgrqnl', qb, kg).astype(F32) * scale - slopes[:, :, None, None, None] * dist_s
        p_s = masked_softmax(ss, (s_pos <= t[:, None, None])[:, :, None], axis=(-2, -1))
        o_sel = jnp.einsum('bgrqnl,bgqnld->bgrqd', p_s, vg)
        kwb = lax.dynamic_slice_in_dim(k_win, t0, WINDOW + NSA_QC, axis=2)
        vwb = lax.dynamic_slice_in_dim(v_win, t0, WINDOW + NSA_QC, axis=2).astype(F32)
        pos = t0 - WINDOW + jnp.arange(WINDOW + NSA_QC)
        sw = jnp.einsum('bgrqd,bgkd->bgrqk', qb, kwb).astype(F32) * scale - slopes[:, :, None, None] * (tf[:, None] - pos.astype(F32)[None, :])
        valid_w = (pos[None, :] >= 0) & (pos[None, :] <= t[:, None]) & (t[:, None] - pos[None, :] < WINDOW)
        o_win = jnp.einsum('bgrqk,bgkd->bgrqd', masked_softmax(sw, valid_w), vwb)
        gb = lax.dynamic_slice_in_dim(g, t0, NSA_QC, axis=3)
        return gb[..., 0:1] * o_cmp + gb[..., 1:2] * o_sel + gb[..., 2:3] * o_win

    out = lax.map(query_chunk, jnp.arange(S // NSA_QC))
    return out.transpose(1, 0, 4, 2, 3, 5).reshape(B, S, NSA_W).astype(dtype)


def hybrid_mixer(h, w_in, gm_ln_g, gm_ln_b, gm_ws, gm_bs, ml_conv_w, ml_conv_b, ml_gate_b, ml_norm_g,
                 nsa_pe_k, nsa_pe_v, nsa_phi_k1, nsa_phi_k2, nsa_phi_v1, nsa_phi_v2, w_up_a, w_up_b, w_up_c, w_out):
    z = h @ w_in
    (gu, gv, mq, mk, mv, mo, mi, mf, nq, nkc, nvc, nks, nvs, nkw, nvw, ngate, ga, gbr, gc) = jnp.split(
        z, np.cumsum(SPLIT_SIZES)[:-1].tolist(), axis=-1)
    y_a = gmlp_mixer(gu, gv, gm_ln_g, gm_ln_b, gm_ws, gm_bs) @ w_up_a
    y_b = mlstm_mixer(mq, mk, mv, mo, mi, mf, ml_conv_w, ml_conv_b, ml_gate_b, ml_norm_g) @ w_up_b
    y_c = nsa_mixer(nq, nkc, nvc, nks, nvs, nkw, nvw, ngate, nsa_pe_k, nsa_pe_v,
                    nsa_phi_k1, nsa_phi_k2, nsa_phi_v1, nsa_phi_v2) @ w_up_c
    merged = jax.nn.sigmoid(ga) * y_a + jax.nn.sigmoid(gbr) * y_b + jax.nn.sigmoid(gc) * y_c
    return merged @ w_out


def setup_inputs(seed: int = 0) -> dict:
    key = jax.random.key(seed)
    ks = jax.random.split(key, 40)
    L = DEPTH

    def nrm(k, shape, scale):
        return scale * jax.random.normal(k, shape, F32)

    ml_gate_b = jnp.concatenate([nrm(ks[12], (L, ML_H), 0.1),
                                 jnp.linspace(3.0, 6.0, ML_H, dtype=F32)[None, :] + nrm(ks[13], (L, ML_H), 0.1)], axis=-1)
    return {
        'x': nrm(ks[0], (BATCH, SEQ, D_MODEL), 1.0),
        'c': nrm(ks[1], (BATCH, D_MODEL), 1.0),
        'g_norm1': 1.0 + nrm(ks[2], (L, D_MODEL), 0.02),
        'g_norm2': 1.0 + nrm(ks[3], (L, D_MODEL), 0.02),
        'w_ada': nrm(ks[4], (L, D_MODEL, 6 * D_MODEL), 0.3 * D_MODEL ** -0.5),
        'b_ada': nrm(ks[5], (L, 6 * D_MODEL), 0.02),
        'w_in': nrm(ks[6], (L, D_MODEL, P_TOTAL), D_MODEL ** -0.5),
        'gm_ln_g': 1.0 + nrm(ks[7], (L, GM_W), 0.02),
        'gm_ln_b': nrm(ks[8], (L, GM_W), 0.02),
        'gm_ws': nrm(ks[9], (L, GM_GROUPS, GM_CHUNK, GM_CHUNK), GM_CHUNK ** -0.5),
        'gm_bs': 1.0 + nrm(ks[10], (L, GM_GROUPS, GM_CHUNK), 0.1),
        'ml_conv_w': nrm(ks[11], (L, CONV_K, 2 * ML_W), CONV_K ** -0.5),
        'ml_conv_b': nrm(ks[14], (L, 2 * ML_W), 0.02),
        'ml_gate_b': ml_gate_b,
        'ml_norm_g': 1.0 + nrm(ks[15], (L, ML_W), 0.02),
        'nsa_pe_k': nrm(ks[16], (L, CMP_BLOCK, NSA_DH), 0.1),
        'nsa_pe_v': nrm(ks[17], (L, CMP_BLOCK, NSA_DH), 0.1),
        'nsa_phi_k1': nrm(ks[18], (L, CMP_BLOCK * NSA_DH, NSA_DH), (CMP_BLOCK * NSA_DH) ** -0.5),
        'nsa_phi_k2': nrm(ks[19], (L, NSA_DH, NSA_DH), NSA_DH ** -0.5),
        'nsa_phi_v1': nrm(ks[20], (L, CMP_BLOCK * NSA_DH, NSA_DH), (CMP_BLOCK * NSA_DH) ** -0.5),
        'nsa_phi_v2': nrm(ks[21], (L, NSA_DH, NSA_DH), NSA_DH ** -0.5),
        'w_up_a': nrm(ks[22], (L, GM_W, D_MODEL), GM_W ** -0.5),
        'w_up_b': nrm(ks[23], (L, ML_W, D_MODEL), ML_W ** -0.5),
        'w_up_c': nrm(ks[24], (L, NSA_W, D_MODEL), NSA_W ** -0.5),
        'w_out': nrm(ks[25], (L, D_MODEL, D_MODEL), D_MODEL ** -0.5),
        'w_mlp1': nrm(ks[26], (L, D_MODEL, D_FF), D_MODEL ** -0.5),
        'w_mlp2': nrm(ks[27], (L, D_FF, D_MODEL), D_FF ** -0.5),
        'g_final': 1.0 + nrm(ks[28], (D_MODEL,), 0.02),
    }


def reference(x, c, g_norm1, g_norm2, w_ada, b_ada, w_in, gm_ln_g, gm_ln_b, gm_ws, gm_bs,
              ml_conv_w, ml_conv_b, ml_gate_b, ml_norm_g, nsa_pe_k, nsa_pe_v, nsa_phi_k1, nsa_phi_k2,
              nsa_phi_v1, nsa_phi_v2, w_up_a, w_up_b, w_up_c, w_out, w_mlp1, w_mlp2, g_final):
    cond = jax.nn.silu(c)
    for l in range(DEPTH):
        mod = (cond @ w_ada[l] + b_ada[l])[:, None, :]
        sh1, sc1, gt1, sh2, sc2, gt2 = jnp.split(mod, 6, axis=-1)
        h = rms_norm(x, g_norm1[l]) * (1.0 + sc1) + sh1
        x = x + gt1 * hybrid_mixer(h, w_in[l], gm_ln_g[l], gm_ln_b[l], gm_ws[l], gm_bs[l],
                                   ml_conv_w[l], ml_conv_b[l], ml_gate_b[l], ml_norm_g[l],
                                   nsa_pe_k[l], nsa_pe_v[l], nsa_phi_k1[l], nsa_phi_k2[l],
                                   nsa_phi_v1[l], nsa_phi_v2[l], w_up_a[l], w_up_b[l], w_up_c[l], w_out[l])
        h = rms_norm(x, g_norm2[l]) * (1.0 + sc2) + sh2
        x = x + gt2 * (jnp.square(jax.nn.relu(h @ w_mlp1[l])) @ w_mlp2[l])
    return rms_norm(x, g_final)
```

```python
import contextlib
import numpy as np
import concourse.bass as bass
import concourse.mybir as mybir
from concourse.bass_utils import run_bass_kernel_spmd

F32 = mybir.dt.float32
BF16 = mybir.dt.bfloat16
AF = mybir.ActivationFunctionType
ALU = mybir.AluOpType
AX = mybir.AxisListType

S = 2048
D = 1024
DFF = 4096
NT = 16
NG = 4
PT = 7456
EPS = 1e-6
C_GU, C_GV, C_MQ, C_MK, C_MV, C_MO, C_MI, C_MF = 0, 512, 1024, 1536, 2048, 2560, 3072, 3076
C_NQ, C_NKC, C_NVC, C_NKS, C_NVS, C_NKW, C_NVW, C_NG = 3080, 3592, 3720, 3848, 3976, 4104, 4232, 4360
C_GA, C_GB, C_GC = 4384, 5408, 6432


class Buf:
    __slots__ = ("name", "w", "r")

    def __init__(self, name=""):
        self.name = name
        self.w = None
        self.r = {}


class Eng:
    def __init__(self, prog, name, h, sem):
        self.name = name
        self.h = h
        self.sem = sem
        self.cnt = 0
        self.waited = {}


class Prog:
    def __init__(self, nc, st, n_dma_sems=40):
        self.nc = nc
        self.sems = {}
        mk = lambda n: st.enter_context(nc.semaphore(n))
        self.pe = Eng(self, "pe", nc.tensor, mk("s_pe"))
        self.act = Eng(self, "act", nc.scalar, mk("s_act"))
        self.dve = Eng(self, "dve", nc.vector, mk("s_dve"))
        self.pool = Eng(self, "pool", nc.gpsimd, mk("s_pool"))
        self.sp = Eng(self, "sp", nc.sync, mk("s_sp"))
        for e in (self.pe, self.act, self.dve, self.pool, self.sp):
            self.sems[e.name] = e.sem
        self.dsems = []
        for i in range(n_dma_sems):
            s = mk("s_d%d" % i)
            self.sems["d%d" % i] = s
            self.dsems.append(["d%d" % i, 0])
        self.dpools = {"sw": [0, list(range(0, n_dma_sems // 2))], "hw": [0, list(range(n_dma_sems // 2, n_dma_sems))]}
        self.nins = 0

    def _wait(self, eng, deps):
        need = {}
        for d in deps:
            if d is None:
                continue
            k, v = d
            if v > need.get(k, 0):
                need[k] = v
        for k, v in need.items():
            if v > eng.waited.get(k, 0):
                eng.h.wait_ge(self.sems[k], v)
                eng.waited[k] = v

    def _deps(self, eng, reads, writes, pe_accum):
        deps = []
        for b in reads:
            deps.append(b.w)
        for b in writes:
            if not ((pe_accum or eng.name == "pe") and b.w is not None and b.w[0] == "pe"):
                deps.append(b.w)
            for k, v in b.r.items():
                deps.append((k, v))
        return deps

    def _mark(self, tk, reads, writes):
        k, v = tk
        for b in reads:
            if v > b.r.get(k, 0):
                b.r[k] = v
        for b in writes:
            b.w = tk
            b.r = {}

    def op(self, eng, fn, reads=(), writes=(), pe_accum=False):
        self._wait(eng, self._deps(eng, reads, writes, pe_accum))
        ins = fn(eng.h)
        eng.cnt += 1
        ins.then_inc(eng.sem, 1)
        self.nins += 1
        self._mark((eng.name, eng.cnt), reads, writes)

    def dma(self, q, out, in_, reads=(), writes=(), **kw):
        deps = self._deps(q, reads, writes, False)
        pl = self.dpools["sw" if q is self.pool else "hw"]
        ds = self.dsems[pl[1][pl[0] % len(pl[1])]]
        pl[0] += 1
        if ds[1] > 0:
            deps.append((ds[0], 16 * ds[1]))
        self._wait(q, deps)
        q.h.dma_start(out=out, in_=in_, **kw).then_inc(self.sems[ds[0]], 16)
        ds[1] += 1
        self.nins += 1
        tk = (ds[0], 16 * ds[1])
        self._mark(tk, reads, writes)
        return tk

    def finish(self, bufs):
        deps = []
        for b in bufs:
            deps.append(b.w)
            deps.extend(b.r.items())
        for ds in self.dsems:
            if ds[1] > 0:
                deps.append((ds[0], 16 * ds[1]))
        for e in (self.pe, self.act, self.dve, self.pool):
            deps.append((e.name, e.cnt))
        self._wait(self.sp, deps)


class PsumPool:
    def __init__(self, nc, n=8, ngen=6):
        self.t = [nc.alloc_psum_tensor("ps%d" % i, [128, 512], F32) for i in range(n)]
        self.b = [Buf("ps%d" % i) for i in range(n)]
        self.i = 0
        self.n = ngen
        self.j = 0
        self.nacc = n - ngen

    def get(self):
        i = self.i
        self.i = (self.i + 1) % self.n
        return self.t[i], self.b[i]

    def get_acc(self):
        j = self.n + self.j
        self.j = (self.j + 1) % self.nacc
        return self.t[j], self.b[j]


class Rot:
    def __init__(self, nc, name, shape, dtype, n):
        self.t = [nc.alloc_sbuf_tensor("%s%d" % (name, i), shape, dtype) for i in range(n)]
        self.b = [Buf("%s%d" % (name, i)) for i in range(n)]
        self.i = 0
        self.n = n

    def get(self):
        i = self.i
        self.i = (self.i + 1) % self.n
        return self.t[i], self.b[i]


class Rot2:
    def __init__(self, tiles):
        self.t = tiles
        self.b = [Buf("r%d" % i) for i in range(len(tiles))]
        self.i = 0

    def get(self):
        i = self.i
        self.i = (self.i + 1) % len(self.t)
        return self.t[i], self.b[i]


class K:
    def __init__(self, nseq, nlayers, mixers="abc", taps=()):
        self.nseq, self.L, self.mixers, self.taps = nseq, nlayers, mixers, taps
        self.nc = nc = bass.Bass("TRN2", target_bir_lowering=False)
        self.st = contextlib.ExitStack()
        self.uid = 0
        self.out_bufs = []

    def dram_in(self, name, shape, dt=F32):
        return self.nc.dram_tensor(name, list(shape), dt, kind="ExternalInput").ap()

    def sb(self, st, name, shape, dt):
        self.uid += 1
        return st.enter_context(self.nc.sbuf_tensor("%s_%d" % (name, self.uid), list(shape), dt))

    def barrier(self):
        p = self.p
        engs = (p.pe, p.act, p.dve, p.pool, p.sp)
        deps = [(e.name, e.cnt) for e in engs if e.cnt > 0]
        hw = set(p.dpools["hw"][1])
        for i, ds in enumerate(p.dsems):
            if ds[1] > 0 and i in hw:
                deps.append((ds[0], 16 * ds[1]))
        for e in engs:
            p._wait(e, deps)

    @contextlib.contextmanager
    def scope(self):
        with contextlib.ExitStack() as s:
            yield s
            self.barrier()

    def prefetch(self, key, src_ap):
        if not hasattr(self, "_pf"):
            self._pf = {}
        self._pf[key] = self.wslab(src_ap)

    def wslab(self, src_ap, shape=None, key=None):
        if key is not None and getattr(self, "_pf", None) and key in self._pf:
            return self._pf.pop(key)
        t, b = self.wrot.get()
        shp = list(src_ap.shape)
        np_ = shp[0]
        if len(shp) == 3:
            dst = t[0:np_, 0:shp[1], 0:shp[2]] if shp[2] == 512 else \
                t[0:np_].rearrange("p a b -> p (a b)")[:, 0:shp[1] * shp[2]].rearrange("p (a b) -> p a b", b=shp[2])
        else:
            dst = t[0:np_].rearrange("p a b -> p (a b)")[:, 0:shp[1]]
        self.p.dma(self.p.pool, dst, src_ap, reads=list(getattr(self, "_wdeps", [])), writes=[b])
        return dst, b

    def mm(self, out, lhsT, rhs, start, stop, reads, wbuf, skip=False):
        self.p.op(self.p.pe, lambda e: e.matmul(out, lhsT, rhs, start=start, stop=stop, skip_group_check=skip),
                  reads=reads, writes=[wbuf], pe_accum=not start)

    def tap(self, name, ap, buf, shape):
        if name not in self.taps:
            return
        d = self.nc.dram_tensor("tap_" + name, list(shape), ap.dtype, kind="ExternalOutput").ap()
        b = Buf("tap")
        self.p.dma(self.p.sp, d, ap, reads=[buf], writes=[b])
        self.out_bufs.append(b)

    def build(self):
        nc, nseq, L = self.nc, self.nseq, self.L
        st = self.st
        self.p = p = Prog(nc, st)
        di = self.dram_in
        self.x = di("x", [nseq * S, D])
        self.cT = di("cT", [128, 8, nseq])
        self.w_ada = di("w_ada", [L, D, 6 * D])
        self.b_adaT = di("b_adaT", [128, L, 48])
        self.w_in = di("w_in", [L, D, PT])
        self.gnT = di("gnT", [128, 2 * L + 1, 8])
        self.w_up = [di("w_up_" + m, [L, 512, D]) for m in "abc"]
        self.w_out = di("w_out", [L, D, D])
        self.w_mlp1 = di("w_mlp1", [L, D, DFF])
        self.w_mlp2 = di("w_mlp2", [L, DFF, D])
        self.consts = di("consts", [128, NCONST])
        self.declare_mixer_inputs()
        self.out = nc.dram_tensor("out", [nseq * S, D], F32, kind="ExternalOutput").ap()
        self.xs = nc.dram_tensor("xs", [nseq, 128, 8, S], F32, kind="Internal").ap()
        self.wf1 = nc.dram_tensor("wf1", [L, D, DFF], BF16, kind="Internal").ap()
        self.wf2 = nc.dram_tensor("wf2", [L, DFF, D], BF16, kind="Internal").ap()
        self.wfb = [[[], []] for l in range(L)]
        self.xsb = [[Buf("xs%d_%d" % (b, g)) for g in range(NG)] for b in range(nseq)]

        sbp = lambda name, shape, dt: nc.alloc_sbuf_tensor(name, list(shape), dt)
        self.ps = PsumPool(nc, 8)
        self.wrot = Rot(nc, "wsl", [128, 8, 512], BF16, 3)
        self.f32t = Rot(nc, "f32t", [128, 512], F32, 5)
        self.bf16t = Rot(nc, "bf16t", [128, 512], BF16, 3)
        self.cst = sbp("cst", [128, NCONST], F32)
        self.cstb = Buf("cst")
        p.dma(p.sp, self.cst[:], self.consts, writes=[self.cstb])
        self.ident = self.cst[:, CO_ID:CO_ID + 128]
        self.identb = sbp("identb", [128, 128], BF16)
        self.onesb = sbp("onesb", [128, 128], BF16)
        p.op(p.dve, lambda e: e.tensor_copy(self.identb[:], self.ident), reads=[self.cstb], writes=[self.cstb])
        p.op(p.dve, lambda e: e.memset(self.onesb[:], 1.0), writes=[self.cstb])
        self.gn = sbp("gn", [128, 2 * L + 1, 8], F32)
        p.dma(p.sp, self.gn[:], self.gnT, writes=[self.cstb])
        self.hbuf = sbp("hbuf", [128, 8, S], BF16)
        self.hb = [Buf("h%d" % g) for g in range(NG)]
        self.setup_mixer_consts(sbp)

        self.phase0(sbp)
        self.tap("modT", self.modT[:], self.modb, [128, L, nseq, 48])
        self.tap("A", self.A[:], self.modb, [128, L, nseq, 2, 8])
        for b in range(nseq):
            if b == 0:
                self.x0(b)
            for l in range(L):
                wvl = self.w_in[l].rearrange("(k p) c -> p k c", p=128)
                with self.scope() as s1:
                    outs = {}
                    for m in self.mixers:
                        outs[m] = (self.sb(s1, "out" + m, [128, 4, S], BF16), [Buf("o%s%d" % (m, g)) for g in range(NG)])
                    if "a" in self.mixers:
                        self.gmlp(b, l, *outs["a"])
                    if b == 0:
                        self.convert_ffn(l)
                    if "b" in self.mixers:
                        self.mlstm(b, l, *outs["b"])
                    if "c" in self.mixers:
                        self.nsa(b, l, *outs["c"])
                    if b == 0 and l == 0:
                        for m in self.mixers:
                            self.tap("out" + m, outs[m][0][:], outs[m][1][NG - 1], [128, 4, S])
                    self.stage1(b, l, outs, s1)
                    if b == 0 and l == 0:
                        self.tap("h2", self.hbuf[:, :, 0:512], self.hb[0], [128, 8, 512])
                with self.scope() as s2:
                    self.stage2(b, l, s2)
        p.finish(self.out_bufs)
        st.close()
        return nc

    def convert_ffn(self, l):
        p = self.p
        for r0 in range(0, D, 128):
            bb = Buf("wf1")
            p.dma(p.pool, self.wf1[l][r0:r0 + 128, :], self.w_mlp1[l][r0:r0 + 128, :], writes=[bb])
            self.wfb[l][0].append(bb)
        for r0 in range(0, DFF, 256):
            bb = Buf("wf2")
            p.dma(p.pool, self.wf2[l][r0:r0 + 256, :], self.w_mlp2[l][r0:r0 + 256, :], writes=[bb])
            self.wfb[l][1].append(bb)

    def phase0(self, sbp):
        p, nseq, L = self.p, self.nseq, self.L
        self.modT = sbp("modT", [128, L, nseq, 48], F32)
        self.A = sbp("Amod", [128, L, nseq, 2, 8], F32)
        self.modb = Buf("mod")
        bad = sbp("bad", [128, L, 48], F32)
        cTf = sbp("cTf", [128, 8, nseq], F32)
        cTb = sbp("cTb", [128, 8, nseq], BF16)
        p.dma(p.sp, bad[:], self.b_adaT, writes=[self.modb])
        p.dma(p.sp, cTf[:], self.cT, writes=[self.modb])
        p.op(p.act, lambda e: e.activation(out=cTb[:], in_=cTf[:], func=AF.Silu), reads=[self.modb], writes=[self.modb])
        for l in range(L):
            wv = self.w_ada[l].rearrange("(k p) c -> p k c", p=128)
            for cg in range(12):
                slab, sbuf = self.wslab(wv[:, :, cg * 512:(cg + 1) * 512])
                for j in range(4):
                    jj = cg * 4 + j
                    pt, pb = self.ps.get()
                    for k in range(8):
                        self.mm(pt[:, 0:nseq], slab[:, k, j * 128:(j + 1) * 128], cTb[:, k, :], k == 0, k == 7,
                                [sbuf, self.modb], pb)
                    p.op(p.dve, lambda e: e.tensor_scalar(out=self.modT[:, l, :, jj], in0=pt[:, 0:nseq],
                                                          scalar1=bad[:, l, jj:jj + 1], scalar2=None, op0=ALU.add),
                         reads=[pb, self.modb], writes=[self.modb])
            for b in range(nseq):
                for w, off in ((0, 8), (1, 32)):
                    p.op(p.dve, lambda e: e.scalar_tensor_tensor(
                        out=self.A[:, l, b, w, :], in0=self.modT[:, l, b, off:off + 8], scalar=1.0,
                        in1=self.gn[:, 2 * l + w, :], op0=ALU.add, op1=ALU.mult),
                        reads=[self.modb, self.cstb], writes=[self.modb])

    def mod(self, l, b, which):
        return self.modT[:, l, b, which * 8:(which + 1) * 8]

    def norm_to_h(self, l, b, w, tg, gfinal=False):
        p = self.p
        pt, pb = self.ps.get()
        for k in range(8):
            sq, sqb = self.bf16t.get()
            p.op(p.act, lambda e: e.activation(out=sq[:], in_=self.xT[:, k, :], func=AF.Square),
                 reads=[self.xTb], writes=[sqb])
            self.mm(pt[:], self.onesb[:], sq[:], k == 0, k == 7, [sqb, self.cstb], pb)
        rs, rsb = self.rsrot.get()
        p.op(p.act, lambda e: e.activation(out=rs[:], in_=pt[:], func=AF.Sqrt, scale=1.0 / D, bias=self.epsc),
             reads=[pb, self.cstb], writes=[rsb])
        p.op(p.dve, lambda e: e.reciprocal(out=rs[:], in_=rs[:]), reads=[rsb], writes=[rsb])
        if not hasattr(self, "_t1"):
            self._t1 = 1
            self.tap("rs", rs[:], rsb, [128, 512])
            self.tap("xT0", self.xT[:], self.xTb, [128, 8, 512])
        return rs, rsb

    def norm1(self, l, b, w, tg):
        p = self.p
        rs, rsb = self.norm_to_h(l, b, w, tg)
        for k in range(8):
            tm, tmb = self.f32t.get()
            p.op(p.dve, lambda e: e.tensor_tensor(out=tm[:], in0=self.xT[:, k, :], in1=rs[:], op=ALU.mult),
                 reads=[self.xTb, rsb], writes=[tmb])
            p.op(p.act, lambda e: e.activation(out=self.hbuf[:, k, tg * 512:(tg + 1) * 512], in_=tm[:],
                                               func=AF.Identity, scale=self.A[:, l, b, w, k:k + 1],
                                               bias=self.modT[:, l, b, 24 * w + k:24 * w + k + 1]),
                 reads=[tmb, self.modb], writes=[self.hb[tg]])

    def x0(self, b):
        p = self.p
        with self.scope() as sx:
            self._x0(b, sx)

    @contextlib.contextmanager
    def use_x(self, xT, xTb):
        old = (self.xT, self.xTb)
        self.xT, self.xTb = xT, xTb
        try:
            yield
        finally:
            self.xT, self.xTb = old

    def alloc_x(self, sx):
        self.xT = self.sb(sx, "xT", [128, 8, 512], F32)
        self.xTb = Buf("xT")
        self.rsrot = Rot2([self.sb(sx, "rs", [128, 512], F32) for _ in range(2)])

    def _x0(self, b, sx):
        self.alloc_x(sx)
        xins = [self.sb(sx, "xin", [128, 1024], F32) for _ in range(2)]
        xinbs = [Buf("xin0"), Buf("xin1")]
        for tg in range(NG):
            self._x0_tg(b, tg, xins, xinbs)

    def _x0_load(self, b, tg, xins, xinbs):
        for tt in range(4):
            t = tg * 4 + tt
            self.p.dma(self.p.sp, xins[tt][:], self.x[b * S + t * 128: b * S + (t + 1) * 128, :], writes=[xinbs[tt]])

    def _x0_tg(self, b, tg, xins, xinbs, preloaded=False):
        p = self.p
        for tt in range(4):
            t = tg * 4 + tt
            xin, xinb = (xins[tt], xinbs[tt]) if preloaded else (xins[t % 2], xinbs[t % 2])
            if not preloaded:
                p.dma(p.sp, xin[:], self.x[b * S + t * 128: b * S + (t + 1) * 128, :], writes=[xinb])
            for half in range(2):
                pt, pb = self.ps.get()
                for j in range(4):
                    k = half * 4 + j
                    p.op(p.pe, lambda e: e.transpose(out=pt[:, j * 128:(j + 1) * 128], in_=xin[:, k * 128:(k + 1) * 128],
                                                     identity=self.ident),
                         reads=[xinb, self.cstb], writes=[pb], pe_accum=(j > 0))
                src = pt[:].rearrange("p (j t) -> p j t", j=4)
                dst = self.xT[:, half * 4:half * 4 + 4, tt * 128:(tt + 1) * 128]
                if half == 0:
                    p.op(p.act, lambda e: e.copy(out=dst, in_=src), reads=[pb], writes=[self.xTb])
                else:
                    p.op(p.dve, lambda e: e.tensor_copy(out=dst, in_=src), reads=[pb], writes=[self.xTb])
        p.dma(p.sp, self.xs[b][:, :, tg * 512:(tg + 1) * 512], self.xT[:], reads=[self.xTb], writes=[self.xsb[b][tg]])
        self.norm1(0, b, 0, tg)

    def stage1(self, b, l, outs, s1):
        p = self.p
        mgh = self.sb(s1, "mgh", [128, 8, S], BF16)
        mghb = [Buf("mgh%d" % g) for g in range(NG)]
        with self.scope() as smg:
            self._merge(b, l, outs, smg, mgh, mghb)
        self.alloc_x(s1)
        xTs = [(self.xT, self.xTb), (self.sb(s1, "xTb2", [128, 8, 512], F32), Buf("xT2"))]
        wo = self.w_out[l].rearrange("(k p) n -> p k n", p=128)
        wsl = [self.wslab(wo[:, :, half * 512:(half + 1) * 512]) for half in range(2)]

        def wres(tg):
            tsl = slice(tg * 512, (tg + 1) * 512)
            xT, xTb = xTs[tg % 2]
            p.dma(p.sp, xT[:], self.xs[b][:, :, tsl], reads=[self.xsb[b][tg]], writes=[xTb])
            for half in range(2):
                ws_, wsb = wsl[half]
                for j in range(4):
                    n = half * 4 + j
                    py, pyb = self.ps.get()
                    for k in range(8):
                        self.mm(py[:], ws_[:, k, j * 128:(j + 1) * 128], mgh[:, k, tsl], k == 0, k == 7, [wsb, mghb[tg]], pyb)
                    p.op(p.dve, lambda e: e.scalar_tensor_tensor(out=xT[:, n, :], in0=py[:], scalar=self.modT[:, l, b, 16 + n:17 + n],
                                                                 in1=xT[:, n, :], op0=ALU.mult, op1=ALU.add),
                         reads=[pyb, self.modb, xTb], writes=[xTb])

        wres(0)
        for tg in range(NG):
            tsl = slice(tg * 512, (tg + 1) * 512)
            if tg + 1 < NG:
                wres(tg + 1)
            xT, xTb = xTs[tg % 2]
            p.dma(p.sp, self.xs[b][:, :, tsl], xT[:], reads=[xTb], writes=[self.xsb[b][tg]])
            with self.use_x(xT, xTb):
                self.norm1(l, b, 1, tg)
        w1v = self.wf1[l].rearrange("(k p) c -> p k c", p=128)
        self._wdeps = self.wfb[l][0]
        for fs in range(2):
            self.prefetch(("w1", b, l, 0, fs), w1v[:, :, fs * 512:(fs + 1) * 512])
        self._wdeps = []

    def _merge(self, b, l, outs, smg, mgh, mghb):
        p = self.p
        mg = self.sb(smg, "mg", [128, 4, S], F32)
        mgb = [Buf("mg%d" % g) for g in range(NG)]
        wv = self.w_in[l].rearrange("(k p) c -> p k c", p=128)
        active = [(mi, m) for mi, m in enumerate("abc") if m in self.mixers]
        if not active:
            p.op(p.dve, lambda e: e.memset(mgh[:], 0.0), writes=mghb)
        for half in range(2):
            if not active:
                break
            first = True
            for mi, m in active:
                oT, ob = outs[m]
                gcol = (C_GA, C_GB, C_GC)[mi]
                kk = (mi == active[0][0] and half == 0)
                gs, gsb = self.wslab(wv[:, :, gcol + half * 512: gcol + (half + 1) * 512], key=("mg_gs", b, l) if kk else None)
                wup, wupb = self.wslab(self.w_up[mi][l].rearrange("(k p) n -> p k n", p=128)[:, :, half * 512:(half + 1) * 512],
                                       key=("mg_up", b, l) if kk else None)
                for tg in range(NG):
                    tsl = slice(tg * 512, (tg + 1) * 512)
                    for j in range(4):
                        py, pyb = self.ps.get()
                        for k in range(4):
                            self.mm(py[:], wup[:, k, j * 128:(j + 1) * 128], oT[:, k, tsl], k == 0, k == 3, [wupb, ob[tg]], pyb)
                        pg, pgb = self.ps.get()
                        for k in range(8):
                            self.mm(pg[:], gs[:, k, j * 128:(j + 1) * 128], self.hbuf[:, k, tsl], k == 0, k == 7, [gsb, self.hb[tg]], pgb)
                        sg, sgb = self.f32t.get()
                        p.op(p.act, lambda e: e.activation(out=sg[:], in_=pg[:], func=AF.Sigmoid), reads=[pgb], writes=[sgb])
                        if first:
                            p.op(p.dve, lambda e: e.tensor_tensor(out=mg[:, j, tsl], in0=py[:], in1=sg[:], op=ALU.mult),
                                 reads=[pyb, sgb], writes=[mgb[tg]])
                        else:
                            p.op(p.dve, lambda e: e.tensor_tensor(out=sg[:], in0=py[:], in1=sg[:], op=ALU.mult),
                                 reads=[pyb, sgb], writes=[sgb])
                            p.op(p.dve, lambda e: e.tensor_tensor(out=mg[:, j, tsl], in0=mg[:, j, tsl], in1=sg[:], op=ALU.add),
                                 reads=[sgb, mgb[tg]], writes=[mgb[tg]])
                first = False
            for tg in range(NG):
                tsl = slice(tg * 512, (tg + 1) * 512)
                p.op(p.act, lambda e: e.copy(out=mgh[:, half * 4:(half + 1) * 4, tsl], in_=mg[:, :, tsl]), reads=[mgb[tg]], writes=[mghb[tg]])

    def stage2(self, b, l, s2):
        p = self.p
        self.alloc_x(s2)
        aT = self.sb(s2, "aT", [128, 32, 512], BF16)
        aTb = Buf("aT")
        self.xos = [self.sb(s2, "xo", [128, 1024], F32) for _ in range(2)]
        self.xobs = [Buf("xo0"), Buf("xo1")]
        prefetch_next = (l == self.L - 1) and (b + 1 < self.nseq)
        if prefetch_next:
            nxT = self.sb(s2, "nxT", [128, 8, 512], F32)
            nxTb = Buf("nxT")
            nxins = [self.sb(s2, "nxin", [128, 1024], F32) for _ in range(4)]
            nxinbs = [Buf("nxin%d" % i) for i in range(4)]
        last = (l == self.L - 1)
        w1v = self.wf1[l].rearrange("(k p) c -> p k c", p=128)
        w2v = self.wf2[l].rearrange("(f p) n -> p f n", p=128)
        self._wdeps = self.wfb[l][0]

        def w1_phase(tg):
            tsl = slice(tg * 512, (tg + 1) * 512)
            for fs in range(8):
                self._wdeps = self.wfb[l][0]
                slab, slb = self.wslab(w1v[:, :, fs * 512:(fs + 1) * 512], key=("w1", b, l, tg, fs))
                self._wdeps = []
                for j in range(4):
                    fc = fs * 4 + j
                    pt, pb = self.ps.get()
                    for k in range(8):
                        self.mm(pt[:], slab[:, k, j * 128:(j + 1) * 128], self.hbuf[:, k, tsl], k == 0, k == 7, [slb, self.hb[tg]], pb)
                    r, rb = self.f32t.get()
                    p.op(p.act, lambda e: e.activation(out=r[:], in_=pt[:], func=AF.Relu), reads=[pb], writes=[rb])
                    p.op(p.dve, lambda e: e.tensor_tensor(out=aT[:, fc, :], in0=r[:], in1=r[:], op=ALU.mult),
                         reads=[rb], writes=[aTb])

        p.dma(p.sp, self.xT[:], self.xs[b][:, :, 0:512], reads=[self.xsb[b][0]], writes=[self.xTb])
        w1_phase(0)
        for tg in range(NG):
            tsl = slice(tg * 512, (tg + 1) * 512)
            if prefetch_next:
                self._x0_load(b + 1, tg, nxins, nxinbs)
            for fs in range(8):
                self._wdeps = self.wfb[l][1]
                slab, slb = self.wslab(w2v[:, fs * 4:(fs + 1) * 4, :])
                self._wdeps = []
                if fs < 7:
                    for j in range(4):
                        fc = fs * 4 + j
                        for n in range(8):
                            self.mm(self.ps.t[n][:], slab[:, j, n * 128:(n + 1) * 128], aT[:, fc, :], fc == 0, fc == 31,
                                    [slb, aTb], self.ps.b[n])
                else:
                    for n in range(8):
                        for j in range(4):
                            fc = fs * 4 + j
                            self.mm(self.ps.t[n][:], slab[:, j, n * 128:(n + 1) * 128], aT[:, fc, :], fc == 0, fc == 31,
                                    [slb, aTb], self.ps.b[n])
                        p.op(p.dve, lambda e: e.scalar_tensor_tensor(out=self.xT[:, n, :], in0=self.ps.t[n][:], scalar=self.modT[:, l, b, 40 + n:41 + n],
                                                                     in1=self.xT[:, n, :], op0=ALU.mult, op1=ALU.add),
                             reads=[self.ps.b[n], self.modb, self.xTb], writes=[self.xTb])
            if tg + 1 < NG:
                w1_phase(tg + 1)
            if not last:
                p.dma(p.sp, self.xs[b][:, :, tsl], self.xT[:], reads=[self.xTb], writes=[self.xsb[b][tg]])
                self.norm1(l + 1, b, 0, tg)
            else:
                self.final(b, tg)
            if tg + 1 < NG:
                p.dma(p.sp, self.xT[:], self.xs[b][:, :, (tg + 1) * 512:(tg + 2) * 512], reads=[self.xsb[b][tg + 1]], writes=[self.xTb])
            if prefetch_next:
                with self.use_x(nxT, nxTb):
                    self._x0_tg(b + 1, tg, nxins, nxinbs, preloaded=True)
        nb, nl = (b, l + 1) if l + 1 < self.L else (b + 1, 0)
        if nb < self.nseq and "a" in self.mixers:
            wvn = self.w_in[nl].rearrange("(k p) c -> p k c", p=128)
            self.prefetch(("gu", nb, nl), wvn[:, :, C_GU:C_GU + 512])
            self.prefetch(("gv", nb, nl), wvn[:, :, C_GV:C_GV + 512])

    def final(self, b, tg):
        p = self.p
        rs, rsb = self.norm_to_h(0, b, 0, tg)
        for k in range(8):
            p.op(p.dve, lambda e: e.scalar_tensor_tensor(out=self.xT[:, k, :], in0=self.xT[:, k, :], scalar=self.gn[:, 2 * self.L, k:k + 1],
                                                         in1=rs[:], op0=ALU.mult, op1=ALU.mult),
                 reads=[self.xTb, rsb, self.cstb], writes=[self.xTb])
        for tt in range(4):
            t = tg * 4 + tt
            xo, xob = self.xos[t % 2], self.xobs[t % 2]
            for half in range(2):
                pt, pb = self.ps.get()
                for j in range(4):
                    k = half * 4 + j
                    p.op(p.pe, lambda e: e.transpose(out=pt[:, j * 128:(j + 1) * 128], in_=self.xT[:, k, tt * 128:(tt + 1) * 128],
                                                     identity=self.ident),
                         reads=[self.xTb, self.cstb], writes=[pb], pe_accum=(j > 0))
                if half == 0:
                    p.op(p.act, lambda e: e.copy(out=xo[:, 0:512], in_=pt[:]), reads=[pb], writes=[xob])
                else:
                    p.op(p.dve, lambda e: e.tensor_copy(out=xo[:, 512:1024], in_=pt[:]), reads=[pb], writes=[xob])
            ob = Buf("o")
            p.dma(p.sp, self.out[b * S + t * 128: b * S + (t + 1) * 128, :], xo[:], reads=[xob], writes=[ob])
            self.out_bufs.append(ob)

    def declare_mixer_inputs(self):
        di, L = self.dram_in, self.L
        self.gm_ln = di("gm_ln", [128, L, 2, 512])
        self.gm_wsT = di("gm_wsT", [128, L, 4, 128])
        self.gm_bs = di("gm_bs", [1, L * 4 * 128])
        self.ml_par = di("ml_par", [128, L, 8 * 4 + 8 + 8 + 4])
        self.nsa_peT = di("nsa_peT", [64, L, 2, 32])
        self.phi1 = [di("nsa_phi_k1", [L, 2048, 64]), di("nsa_phi_v1", [L, 2048, 64])]
        self.phi2 = [di("nsa_phi_k2", [L, 64, 64]), di("nsa_phi_v2", [L, 64, 64])]
        self.nsa_krows = di("nsa_krows", [4, 2048])
        self.nsa_qrows = di("nsa_qrows", [8, 4, 2048])
        self.nsa_crows = di("nsa_crows", [4, 127])
        self.nsa_cmask = di("nsa_cmask", [128, 2048])
        self.nsa_onehot = di("nsa_onehot", [32, S])
        self.nsa_dmask = di("nsa_dmask", [128, 256])
        self.nsa_ka = di("nsa_ka", [128, 16, 64])
        self.nsa_ovl = di("nsa_ovl", [127, 33])

    def setup_mixer_consts(self, sbp):
        p, L = self.p, self.L
        self.epsc = self.cst[:, CO_EPS:CO_EPS + 1]
        self.tri = self.cst[:, CO_TRI:CO_TRI + 128]
        self.onesf = sbp("onesf", [1, 128], F32)
        p.op(p.dve, lambda e: e.memset(self.onesf[:], 1.0), writes=[self.cstb])
        self.small = Rot(self.nc, "small", [128, 16], F32, 8)
        self.mlp = sbp("mlp", [128, L, 52], F32)
        p.dma(p.sp, self.mlp[:], self.ml_par, writes=[self.cstb])
        self.onec = self.cst[:, CO_ONE:CO_ONE + 1]
        self.lnsc = self.cst[:, CO_LNS:CO_LNS + 1]
        self.onesF = self.cst[:, CO_ONES:CO_ONES + 128]

    def gmlp(self, b, l, oT, ob):
        p = self.p
        wv = self.w_in[l].rearrange("(k p) c -> p k c", p=128)
        with self.scope() as sc:
            uT = [self.sb(sc, "uT", [128, 4, 512], BF16) for _ in range(2)]
            uTb = [Buf("uT0"), Buf("uT1")]
            vt = [self.sb(sc, "gv", [128, 512], F32) for _ in range(8)]
            vtb = [Buf("gv%d" % i) for i in range(8)]
            vh = [self.sb(sc, "gvh", [128, 512], BF16) for _ in range(2)]
            vhb = [Buf("gvh0"), Buf("gvh1")]
            gcb = Buf("gmc")
            gmln = self.sb(sc, "gmln", [128, 2, 512], F32)
            p.dma(p.sp, gmln[:], self.gm_ln[:, l], writes=[gcb])
            wsf = self.sb(sc, "gmwsf", [128, 4, 128], F32)
            p.dma(p.sp, wsf[:], self.gm_wsT[:, l], writes=[gcb])
            gmws = self.sb(sc, "gmws", [128, 4, 128], BF16)
            p.op(p.dve, lambda e: e.tensor_tensor(out=gmws[:], in0=wsf[:], in1=self.tri.unsqueeze(1).broadcast_to([128, 4, 128]), op=ALU.mult),
                 reads=[gcb, self.cstb], writes=[gcb])
            gmbs = self.sb(sc, "gmbs", [1, 512], F32)
            p.dma(p.sp, gmbs[:], self.gm_bs[:, l * 512:(l + 1) * 512], writes=[gcb])
            slab_u, slb_u = self.wslab(wv[:, :, C_GU:C_GU + 512], key=("gu", b, l))
            slab_v, slb_v = self.wslab(wv[:, :, C_GV:C_GV + 512], key=("gv", b, l))
            mvs = {}
            mvt = [self.sb(sc, "gmv", [128, 16], F32) for _ in range(2)]
            mvtb = [Buf("gmv0"), Buf("gmv1")]

            def part1(tg):
                tsl = slice(tg * 512, (tg + 1) * 512)
                u, ub = uT[tg % 2], uTb[tg % 2]
                for g in range(4):
                    pt, pb = self.ps.get()
                    for k in range(8):
                        self.mm(pt[:], slab_u[:, k, g * 128:(g + 1) * 128], self.hbuf[:, k, tsl], k == 0, k == 7, [slb_u, self.hb[tg]], pb)
                    p.op(p.act, lambda e: e.activation(out=u[:, g, :], in_=pt[:], func=AF.Gelu_apprx_tanh), reads=[pb], writes=[ub])
                mv4, mv4b = mvt[tg % 2], mvtb[tg % 2]
                mvs[tg] = (mv4, mv4b)
                for tt in range(4):
                    t = tg * 4 + tt
                    tok = slice(t * 128, (t + 1) * 128)
                    gv, gvb = vt[t % 8], vtb[t % 8]
                    pt, pb = self.ps.get()
                    for k in range(8):
                        self.mm(pt[:], self.hbuf[:, k, tok], slab_v[:, k, :], k == 0, k == 7, [slb_v, self.hb[tg]], pb)
                    p.op(p.act, lambda e: e.activation(out=gv[:], in_=pt[:], func=AF.Gelu_apprx_tanh), reads=[pb], writes=[gvb])
                    sm, smb = self.small.get()
                    p.op(p.dve, lambda e: e.bn_stats(out=sm[:, 0:6], in_=gv[:]), reads=[gvb], writes=[smb])
                    p.op(p.dve, lambda e: e.bn_aggr(out=mv4[:, 2 * tt:2 * tt + 2], in_=sm[:, 0:6]), reads=[smb], writes=[mv4b])

            def part2(tg):
                u, ub = uT[tg % 2], uTb[tg % 2]
                mv4, mv4b = mvs[tg]
                mv3 = mv4[:, 0:8].rearrange("p (t c) -> p t c", c=2)
                p.op(p.act, lambda e: e.activation(out=mv4[:, 8:12], in_=mv3[:, :, 1], func=AF.Sqrt, bias=self.epsc, scale=1.0),
                     reads=[mv4b, self.cstb], writes=[mv4b])
                p.op(p.dve, lambda e: e.reciprocal(out=mv4[:, 12:16], in_=mv4[:, 8:12]), reads=[mv4b], writes=[mv4b])
                for tt in range(4):
                    t = tg * 4 + tt
                    tok = slice(t * 128, (t + 1) * 128)
                    gv, gvb = vt[t % 8], vtb[t % 8]
                    v, vb = vh[t % 2], vhb[t % 2]
                    p.op(p.dve, lambda e: e.tensor_scalar(out=gv[:], in0=gv[:], scalar1=mv4[:, 2 * tt:2 * tt + 1], scalar2=mv4[:, 12 + tt:13 + tt],
                                                          op0=ALU.subtract, op1=ALU.mult), reads=[gvb, mv4b], writes=[gvb])
                    p.op(p.dve, lambda e: e.tensor_tensor(out=gv[:], in0=gv[:], in1=gmln[:, 0, :], op=ALU.mult),
                         reads=[gvb, gcb], writes=[gvb])
                    p.op(p.dve, lambda e: e.tensor_tensor(out=v[:], in0=gv[:], in1=gmln[:, 1, :], op=ALU.add),
                         reads=[gvb, gcb], writes=[vb])
                    pt, pb = self.ps.get()
                    for g in range(4):
                        self.mm(pt[:, g * 128:(g + 1) * 128], v[:, g * 128:(g + 1) * 128], gmws[:, g, :], True, False,
                                [vb, gcb], pb)
                        o = g * 128
                        self.mm(pt[:, g * 128:(g + 1) * 128], self.onesf[0:1, :], gmbs[0:1, o:o + 128], False, True,
                                [self.cstb, gcb], pb)
                    p.op(p.dve, lambda e: e.tensor_tensor(out=oT[:, :, tok], in0=pt[:].rearrange("p (g t) -> p g t", g=4),
                                                          in1=u[:, :, tt * 128:(tt + 1) * 128], op=ALU.mult),
                         reads=[pb, ub], writes=[ob[tg]])

            part1(0)
            for tg in range(NG):
                if tg + 1 < NG:
                    part1(tg + 1)
                part2(tg)
            if "b" in self.mixers:
                self.prefetch(("mi", b, l), wv[:, :, C_MI:C_MI + 8])
                self.prefetch(("mh0", b, l), wv[:, :, C_MQ:C_MQ + 512])

    def mlstm(self, b, l, oT, ob):
        p = self.p
        wv = self.w_in[l].rearrange("(k p) c -> p k c", p=128)
        ALLH = [self.hb[g] for g in range(NG)]
        with self.scope() as sc:
            gt = self.sb(sc, "mlg", [128, 16, 8], F32)
            wk = self.sb(sc, "mlwk", [128, 6, 64], F32)
            row = self.sb(sc, "mlrow", [64, 256], F32)
            GB = self.sb(sc, "mlGB", [128, 128], F32)
            gb = Buf("mlgates")
            slab, slb = self.wslab(wv[:, :, C_MI:C_MI + 8], key=("mi", b, l))
            pg, pgb = self.ps.get()
            for t in range(NT):
                for k in range(8):
                    self.mm(pg[:, t * 8:(t + 1) * 8], self.hbuf[:, k, t * 128:(t + 1) * 128], slab[:, k, 0:8], k == 0, k == 7,
                            [slb, self.hb[t // 4]], pgb)
            p.op(p.dve, lambda e: e.tensor_tensor(out=gt[:], in0=pg[:, 0:128].rearrange("p (t c) -> p t c", c=8),
                                                  in1=self.mlp[:, l, 40:48].unsqueeze(1).broadcast_to([128, 16, 8]), op=ALU.add),
                 reads=[pgb, self.cstb], writes=[gb])
            v3 = lambda i: wk[:, i, :].rearrange("p (t h) -> p t h", h=4)
            p.op(p.act, lambda e: e.activation(out=v3(0), in_=gt[:, :, 4:8], func=AF.Exp, scale=-1.0), reads=[gb], writes=[gb])
            p.op(p.act, lambda e: e.activation(out=v3(0), in_=v3(0), func=AF.Ln, bias=self.onec, scale=1.0), reads=[gb, self.cstb], writes=[gb])
            pw, pwb = self.ps.get()
            self.mm(pw[:, 0:64], self.tri, wk[:, 0, :], True, True, [gb, self.cstb], pwb)
            self.mm(pw[:, 64:128], self.onesF, wk[:, 0, :], True, True, [gb, self.cstb], pwb)
            p.op(p.dve, lambda e: e.tensor_copy(out=wk[:, 1, :], in_=pw[:, 64:128]), reads=[pwb], writes=[gb])
            for h in range(4):
                p.op(p.dve, lambda e: e.tensor_tensor_scan(out=v3(5)[:, :, h], data0=self.onesF[:, 0:16], data1=v3(1)[:, :, h],
                                                           initial=0.0, op0=ALU.mult, op1=ALU.add), reads=[gb, self.cstb], writes=[gb])
            p.op(p.dve, lambda e: e.tensor_tensor(out=wk[:, 5, :], in0=wk[:, 5, :], in1=wk[:, 1, :], op=ALU.subtract), reads=[gb], writes=[gb])
            p.op(p.dve, lambda e: e.tensor_tensor(out=wk[:, 1, :], in0=wk[:, 5, :], in1=pw[:, 0:64], op=ALU.add), reads=[gb, pwb], writes=[gb])
            p.op(p.dve, lambda e: e.tensor_tensor(out=v3(2), in0=gt[:, :, 0:4], in1=v3(1), op=ALU.add), reads=[gb], writes=[gb])
            pt, pb = self.ps.get()
            p.op(p.pe, lambda e: e.transpose(out=pt[0:64, 0:128], in_=wk[:, 2, :], identity=self.ident), reads=[gb, self.cstb], writes=[pb])
            p.op(p.dve, lambda e: e.tensor_reduce(out=row[:, 0:1], in_=pt[0:64, 0:128], axis=AX.X, op=ALU.max), reads=[pb], writes=[gb])
            pt2, pb2 = self.ps.get()
            p.op(p.pe, lambda e: e.transpose(out=pt2[0:1, 0:64], in_=row[:, 0:1], identity=self.ident[0:64, 0:64]), reads=[gb, self.cstb], writes=[pb2])
            p.op(p.dve, lambda e: e.tensor_copy(out=row[0:1, 64:128], in_=pt2[0:1, 0:64]), reads=[pb2], writes=[gb])
            r3 = lambda a: row[0:1, a:a + 64].rearrange("p (t h) -> p t h", h=4)
            for h in range(4):
                p.op(p.dve, lambda e: e.tensor_tensor_scan(out=r3(128)[:, :, h], data0=r3(64)[:, :, h], data1=r3(64)[:, :, h],
                                                           initial=0.0, op0=ALU.max, op1=ALU.max), reads=[gb], writes=[gb])
            p.op(p.dve, lambda e: e.memset(row[0:1, 192:196], 0.0), writes=[gb])
            p.op(p.dve, lambda e: e.tensor_copy(out=row[0:1, 196:256], in_=row[0:1, 128:188]), reads=[gb], writes=[gb])
            p.op(p.dve, lambda e: e.tensor_tensor(out=row[0:1, 192:256], in0=row[0:1, 192:256], in1=row[0:1, 128:192], op=ALU.subtract), reads=[gb], writes=[gb])
            p.op(p.act, lambda e: e.activation(out=row[0:1, 192:256], in_=row[0:1, 192:256], func=AF.Exp), reads=[gb], writes=[gb])
            pt3, pb3 = self.ps.get()
            self.mm(pt3[:, 0:128], self.onesf[0:1, :], row[0:1, 128:256], True, True, [gb, self.cstb], pb3)
            p.op(p.dve, lambda e: e.tensor_copy(out=GB[:], in_=pt3[:, 0:128]), reads=[pb3], writes=[gb])
            p.op(p.dve, lambda e: e.tensor_tensor(out=wk[:, 3, :], in0=wk[:, 2, :], in1=GB[:, 0:64], op=ALU.subtract), reads=[gb], writes=[gb])
            p.op(p.act, lambda e: e.activation(out=wk[:, 3, :], in_=wk[:, 3, :], func=AF.Exp, bias=self.lnsc, scale=1.0), reads=[gb, self.cstb], writes=[gb])
            p.op(p.dve, lambda e: e.tensor_tensor(out=wk[:, 4, :], in0=wk[:, 1, :], in1=GB[:, 0:64], op=ALU.subtract), reads=[gb], writes=[gb])
            p.op(p.act, lambda e: e.activation(out=wk[:, 4, :], in_=wk[:, 4, :], func=AF.Exp), reads=[gb], writes=[gb])
            ev, clv = v3(3), v3(4)
            rv = GB[:, 64:128].rearrange("p (t h) -> p t h", h=4)
            self.tap("mlwk", wk[:], gb, [128, 6, 64])
            self.tap("mlGB", GB[:], gb, [128, 128])
            NSET = 2
            sets = []
            for i in range(NSET):
                d = {}
                d["qk"] = [self.sb(sc, "mlqk", [128, S], BF16) for _ in range(2)]
                d["qkb"] = [Buf("q%d" % i), Buf("k%d" % i)]
                d["vp"] = self.sb(sc, "mlv", [128, NT, 129], BF16)
                d["vpb"] = Buf("v%d" % i)
                d["kt"] = self.sb(sc, "mlkt", [128, NT, 128], BF16)
                d["ktb"] = Buf("kt%d" % i)
                d["num"] = self.sb(sc, "mlnum", [128, NT, 129], F32)
                d["numb"] = Buf("num%d" % i)
                d["D"] = self.sb(sc, "mlD", [128, NT - 1, 129], BF16)
                d["Db"] = Buf("D%d" % i)
                p.op(p.dve, lambda e: e.memset(d["vp"][:, :, 128:129], 1.0), writes=[d["vpb"]])
                sets.append(d)
            pre = [self.sb(sc, "mlpre", [128, 515], BF16) for _ in range(2)]
            dg = self.sb(sc, "mldg", [128, 8, 4, 128], BF16)
            dgb = Buf("dg")
            for c8_ in range(8):
                for j_ in range(4):
                    p.op(p.dve, lambda e: e.tensor_scalar(out=dg[:, c8_, j_, :], in0=self.identb[:], scalar1=self.mlp[:, l, c8_ * 4 + j_:c8_ * 4 + j_ + 1],
                                                          scalar2=None, op0=ALU.mult), reads=[self.cstb], writes=[dgb])
            preb = [Buf("pre0"), Buf("pre1")]
            sq4 = [self.sb(sc, "mlsq4", [128, 4, 128], F32) for _ in range(2)]
            sq4b = [Buf("sq40"), Buf("sq41")]
            NSTM = 4
            stm = [self.sb(sc, "mlstm", [128, 128], BF16) for _ in range(NSTM)]
            stmb = [Buf("stm%d" % i) for i in range(NSTM)]
            sgo = [self.sb(sc, "mlsgo", [128, 512], BF16) for _ in range(2)]
            sgob = [Buf("sgo0"), Buf("sgo1")]
            prei = [0]
            hslabs = {}

            def stage_a(h):
                d = sets[h % NSET]
                hslab, hslb = self.wslab(wv[:, :, C_MQ + h * 512:C_MQ + (h + 1) * 512], key=("mh%d" % h, b, l))
                hslabs[h] = (hslab, hslb)
                for w in range(2):
                    slab, slb = hslab[:, :, w * 128:(w + 1) * 128], hslb
                    c8 = w * 4 + h
                    prev = None
                    for tg in range(NG):
                        st_, stb_ = pre[prei[0] % 2], preb[prei[0] % 2]
                        prei[0] += 1
                        pt, pb = self.ps.get()
                        for k in range(8):
                            self.mm(pt[:], slab[:, k, :], self.hbuf[:, k, tg * 512:(tg + 1) * 512], k == 0, k == 7, [slb, self.hb[tg]], pb)
                        if prev is None:
                            p.op(p.dve, lambda e: e.memset(st_[:, 0:3], 0.0), writes=[stb_])
                        else:
                            p.op(p.act, lambda e: e.copy(out=st_[:, 0:3], in_=prev[0][:, 512:515]), reads=[prev[1]], writes=[stb_])
                        p.op(p.act, lambda e: e.copy(out=st_[:, 3:515], in_=pt[:]), reads=[pb], writes=[stb_])
                        pc_, pcb_ = self.ps.get()
                        for j in range(4):
                            self.mm(pc_[:], dg[:, c8, j, :], st_[:, j:j + 512], j == 0, j == 3, [dgb, stb_], pcb_)
                        p.op(p.act, lambda e: e.activation(out=d["qk"][w][:, tg * 512:(tg + 1) * 512], in_=pc_[:], func=AF.Silu,
                                                           bias=self.mlp[:, l, 32 + c8:33 + c8], scale=1.0), reads=[pcb_, self.cstb], writes=[d["qkb"][w]])
                        prev = (st_, stb_)
                kT = d["qk"][1]
                slab, slb = hslab[:, :, 256:384], hslb
                for t4 in range(4):
                    pt, pb = self.ps.get()
                    for j in range(4):
                        t = t4 * 4 + j
                        for k in range(8):
                            self.mm(pt[:, j * 128:(j + 1) * 128], self.hbuf[:, k, t * 128:(t + 1) * 128], slab[:, k, :], k == 0, k == 7,
                                    [slb, self.hb[t4]], pb)
                    p.op(p.act, lambda e: e.copy(out=d["vp"][:, t4 * 4:(t4 + 1) * 4, 0:128], in_=pt[:].rearrange("p (j c) -> p j c", j=4)),
                         reads=[pb], writes=[d["vpb"]])
                for t4 in range(4):
                    pt, pb = self.ps.get()
                    ptb = pt[:].bitcast(BF16)
                    for j in range(4):
                        t = t4 * 4 + j
                        p.op(p.pe, lambda e: e.transpose(out=ptb[:, j * 128:(j + 1) * 128], in_=kT[:, t * 128:(t + 1) * 128], identity=self.identb[:]),
                             reads=[d["qkb"][1], self.cstb], writes=[pb], pe_accum=(j > 0))
                    for j in range(4):
                        t = t4 * 4 + j
                        p.op(p.dve, lambda e: e.tensor_scalar(out=d["kt"][:, t, :], in0=ptb[:, j * 128:(j + 1) * 128], scalar1=ev[:, t, h:h + 1], scalar2=None,
                                                              op0=ALU.mult), reads=[pb, gb], writes=[d["ktb"]])

            def unpack(h):
                d = sets[h % NSET]
                return d["qk"][0], d["qk"][1], d["qkb"], d["vp"], d["vpb"], d["kt"], d["ktb"], d["num"], d["numb"], d["D"], d["Db"]

            def stage_b1(h):
                qT, kT, qkb, vp, vpb, kt, ktb, numall, numb, Dall, Dallb = unpack(h)
                for c3 in range(5):
                    pd, pdb = self.ps.get()
                    for j in range(3):
                        c = c3 * 3 + j
                        self.mm(pd[:, j * 129:(j + 1) * 129], kt[:, c, :], vp[:, c, :], True, True, [ktb, vpb], pdb)
                    p.op(p.act, lambda e: e.copy(out=numall[:, c3 * 3:(c3 + 1) * 3, :], in_=pd[:, 0:387].rearrange("p (j c) -> p j c", j=3)),
                         reads=[pdb], writes=[numb])
                for c in range(1, NT - 1):
                    p.op(p.dve, lambda e: e.scalar_tensor_tensor(out=numall[:, c, :], in0=numall[:, c - 1, :], scalar=rv[:, c, h:h + 1], in1=numall[:, c, :],
                                                                 op0=ALU.mult, op1=ALU.add), reads=[numb, gb], writes=[numb])
                p.op(p.dve, lambda e: e.tensor_tensor(out=Dall[:], in0=numall[:, 0:NT - 1, :],
                                                       in1=rv[:, 1:NT, h].unsqueeze(2).broadcast_to([128, NT - 1, 129]), op=ALU.mult),
                     reads=[numb, gb], writes=[Dallb])

            def stage_b2(h):
                qT, kT, qkb, vp, vpb, kt, ktb, numall, numb, Dall, Dallb = unpack(h)

                def scores(c):
                    tok = slice(c * 128, (c + 1) * 128)
                    ps_, psb = self.ps.get()
                    self.mm(ps_[:, 0:128], kT[:, tok], qT[:, tok], True, True, [qkb[0], qkb[1]], psb)
                    p.op(p.dve, lambda e: e.scalar_tensor_tensor(out=stm[c % NSTM][:], in0=ps_[:, 0:128], scalar=ev[:, c, h:h + 1], in1=self.tri,
                                                                 op0=ALU.mult, op1=ALU.mult), reads=[psb, gb, self.cstb], writes=[stmb[c % NSTM]])
                nis = 0
                for c in range(NT):
                    while nis < NT and nis <= c + 2:
                        scores(nis)
                        nis += 1
                    tok = slice(c * 128, (c + 1) * 128)
                    pn, pnb = self.ps.get()
                    if c > 0:
                        self.mm(pn[:, 0:129], qT[:, tok], Dall[:, c - 1, :], True, False, [qkb[0], Dallb], pnb)
                    self.mm(pn[:, 0:129], stm[c % NSTM][:], vp[:, c, :], c == 0, True, [stmb[c % NSTM], vpb], pnb)
                    p.op(p.act, lambda e: e.copy(out=numall[:, c, :], in_=pn[:, 0:129]), reads=[pnb], writes=[numb])
                hs = numall[:, :, 0:128]
                sm, smb = self.small.get()
                p.op(p.act, lambda e: e.activation(out=sm[:, 0:16], in_=numall[:, :, 128], func=AF.Abs), reads=[numb], writes=[smb])
                p.op(p.dve, lambda e: e.tensor_tensor(out=sm[:, 0:16], in0=sm[:, 0:16], in1=clv[:, :, h], op=ALU.max), reads=[smb, gb], writes=[smb])
                p.op(p.dve, lambda e: e.reciprocal(out=sm[:, 0:16], in_=sm[:, 0:16]), reads=[smb], writes=[smb])
                p.op(p.dve, lambda e: e.tensor_tensor(out=hs, in0=hs, in1=sm[:, 0:16].unsqueeze(2).broadcast_to([128, NT, 128]), op=ALU.mult),
                     reads=[numb, smb], writes=[numb])
                sm2, sm2b = self.small.get()
                p.op(p.dve, lambda e: e.tensor_reduce(out=sm2[:, 0:16], in_=hs, axis=AX.X, op=ALU.add), reads=[numb], writes=[sm2b])
                p.op(p.dve, lambda e: e.tensor_scalar(out=sm2[:, 0:16], in0=sm2[:, 0:16], scalar1=1.0 / 128, scalar2=None, op0=ALU.mult), reads=[sm2b], writes=[sm2b])
                p.op(p.dve, lambda e: e.tensor_tensor(out=hs, in0=hs, in1=sm2[:, 0:16].unsqueeze(2).broadcast_to([128, NT, 128]), op=ALU.subtract),
                     reads=[numb, sm2b], writes=[numb])
                sm3, sm3b = self.small.get()
                for c4 in range(4):
                    p.op(p.act, lambda e: e.activation(out=sq4[c4 % 2][:], in_=numall[:, c4 * 4:(c4 + 1) * 4, 0:128], func=AF.Square), reads=[numb], writes=[sq4b[c4 % 2]])
                    p.op(p.dve, lambda e: e.tensor_reduce(out=sm3[:, c4 * 4:(c4 + 1) * 4], in_=sq4[c4 % 2][:], axis=AX.X, op=ALU.add), reads=[sq4b[c4 % 2]], writes=[sm3b])
                p.op(p.act, lambda e: e.activation(out=sm3[:, 0:16], in_=sm3[:, 0:16], func=AF.Sqrt, bias=self.epsc, scale=1.0 / 128), reads=[sm3b, self.cstb], writes=[sm3b])
                p.op(p.dve, lambda e: e.reciprocal(out=sm3[:, 0:16], in_=sm3[:, 0:16]), reads=[sm3b], writes=[sm3b])
                p.op(p.dve, lambda e: e.tensor_tensor(out=hs, in0=hs, in1=sm3[:, 0:16].unsqueeze(2).broadcast_to([128, NT, 128]), op=ALU.mult),
                     reads=[numb, sm3b], writes=[numb])
            def stage_b3(h):
                qT, kT, qkb, vp, vpb, kt, ktb, numall, numb, Dall, Dallb = unpack(h)
                hslab, hslb = hslabs[h]
                slab, slb = hslab[:, :, 384:512], hslb
                for t4 in range(4):
                    pg_, pgb_ = self.ps.get()
                    for k in range(8):
                        self.mm(pg_[:], slab[:, k, :], self.hbuf[:, k, t4 * 512:(t4 + 1) * 512], k == 0, k == 7, [slb, self.hb[t4]], pgb_)
                    p.op(p.act, lambda e: e.activation(out=sgo[t4 % 2][:], in_=pg_[:], func=AF.Sigmoid), reads=[pgb_], writes=[sgob[t4 % 2]])
                    pt, pb = self.ps.get()
                    for j in range(4):
                        t = t4 * 4 + j
                        p.op(p.pe, lambda e: e.transpose(out=pt[:, j * 128:(j + 1) * 128], in_=numall[:, t, 0:128], identity=self.ident),
                             reads=[numb, self.cstb], writes=[pb], pe_accum=(j > 0))
                    p.op(p.dve, lambda e: e.scalar_tensor_tensor(out=oT[:, h, t4 * 512:(t4 + 1) * 512], in0=pt[:], scalar=self.mlp[:, l, 48 + h:49 + h],
                                                                 in1=sgo[t4 % 2][:], op0=ALU.mult, op1=ALU.mult),
                         reads=[pb, sgob[t4 % 2], self.cstb], writes=[ob[t4]])

            for st_name, h in (("a", 0), ("b1", 0), ("a", 1), ("b2", 0), ("b1", 1), ("a", 2), ("b3", 0), ("b2", 1), ("b1", 2), ("a", 3),
                               ("b3", 1), ("b2", 2), ("b1", 3), ("b3", 2), ("b2", 3), ("b3", 3)):
                {"a": stage_a, "b1": stage_b1, "b2": stage_b2, "b3": stage_b3}[st_name](h)
            if "c" in self.mixers:
                self.prefetch(("ng", b, l), wv[:, :, C_NG:C_NG + 24])
                self.prefetch(("nkv0", b, l), wv[:, :, C_NKC:C_NKC + 384])

    def nsa(self, b, l, oT, ob):
        p = self.p
        wv = self.w_in[l].rearrange("(k p) c -> p k c", p=128)
        with self.scope() as sc:
            cmask = self.sb(sc, "ncmask", [128, S], BF16)
            eexp = None
            dmask = self.sb(sc, "ndmask", [128, 256], BF16)
            ka = self.sb(sc, "nka", [128, 16, 64], F32)
            gtm = self.sb(sc, "ngtm", [128, NT, 24], F32)
            cb = Buf("nsac")
            p.dma(p.pool, cmask[:], self.nsa_cmask, writes=[cb])
            p.dma(p.pool, dmask[:], self.nsa_dmask, writes=[cb])
            p.dma(p.sp, ka[:], self.nsa_ka, writes=[cb])
            slab, slb = self.wslab(wv[:, :, C_NG:C_NG + 24], key=("ng", b, l))
            pg, pgb = self.ps.get()
            for t in range(NT):
                for k in range(8):
                    self.mm(pg[:, t * 24:(t + 1) * 24], self.hbuf[:, k, t * 128:(t + 1) * 128], slab[:, k, 0:24], k == 0, k == 7,
                            [slb, self.hb[t // 4]], pgb)
            p.op(p.act, lambda e: e.activation(out=gtm[:].rearrange("p t c -> p (t c)"), in_=pg[:, 0:NT * 24], func=AF.Sigmoid), reads=[pgb], writes=[cb])
            cache = {"_scope": sc}
            for g in range(2):
                self.nsa_group(b, l, g, oT, ob, cache, wv, cmask, eexp, dmask, ka, gtm, cb)
            mi0 = [i for i, m in enumerate("abc") if m in self.mixers][0]
            self.prefetch(("mg_gs", b, l), wv[:, :, (C_GA, C_GB, C_GC)[mi0]:(C_GA, C_GB, C_GC)[mi0] + 512])
            self.prefetch(("mg_up", b, l), self.w_up[mi0][l].rearrange("(k p) n -> p k n", p=128)[:, :, 0:512])

    def nsa_group(self, b, l, g, oT, ob, sg, wv, cmask, eexp, dmask, ka, gtm, cb):
        p = self.p
        cache = sg

        def A(name, shape, dt):
            if name not in cache:
                cache[name] = (self.sb(cache["_scope"], name, shape, dt), Buf(name))
            return cache[name]
        qa, qab = zip(*[A("nqa%d" % i, [128, S], BF16) for i in range(4)])
        kaug, kab = zip(*[A("nka%d" % i, [128, S], BF16) for i in range(2)])
        kcaug, kcab = A("nkca", [128, 128], BF16)
        vsw, vswb = A("nvsw", [128, NT, 2, 65], BF16)
        vcp, vcpb = A("nvcp", [127, 97], BF16)
        oacc, oaccb = A("noacc", [128, 4, 4, 64], F32)
        otmp, otmpb = A("notmp", [128, 4, 64], F32)
        impacc, impb = A("nimp", [128, 4, 32], F32)
        imptmp, imptb = A("nimpt", [128, 4, 32], F32)
        mskb, mskbb = A("nmskb", [128, 4, 32], BF16)
        mbT, mbTb = A("nmbT", [32, 512], BF16)
        LOOKAHEAD, NPT = 2, 4
        PT, PTb = zip(*[A("nPT%d" % i, [128, 512], BF16) for i in range(NPT)])
        pti = [0]

        for w in range(2):
            p.op(p.dve, lambda e: e.memset(kaug[w][64:128, :], 0.0), writes=[kab[w]])
            p.dma(p.pool, kaug[w][96:100, :], self.nsa_krows, writes=[kab[w]])
        p.dma(p.pool, kaug[0][64:96, :], self.nsa_onehot, writes=[kab[0]])
        p.op(p.dve, lambda e: e.memset(kcaug[:], 0.0), writes=[kcab])
        p.dma(p.pool, kcaug[96:100, 0:127], self.nsa_crows, writes=[kcab])
        for r in range(4):
            p.op(p.dve, lambda e: e.memset(qa[r][64:128, :], 0.0), writes=[qab[r]])
            p.dma(p.pool, qa[r][96:100, :], self.nsa_qrows[g * 4 + r], writes=[qab[r]])
        p.dma(p.pool, vcp[:, 64:97], self.nsa_ovl, writes=[vcpb])
        p.op(p.dve, lambda e: e.memset(vsw[:, :, :, 64:65], 1.0), writes=[vswb])
        gbase = C_NKC + g * 384
        slab, slb = self.wslab(wv[:, :, gbase:gbase + 384], key=("nkv%d" % g, b, l))
        kvslab, kvslb = slab, slb
        for tg in range(NG):
            pt, pb = self.ps.get()
            for k in range(8):
                self.mm(pt[:], slab[:, k, 0:128], self.hbuf[:, k, tg * 512:(tg + 1) * 512], k == 0, k == 7, [slb, self.hb[tg]], pb)
            p.op(p.act, lambda e: e.copy(out=kaug[0][0:64, tg * 512:(tg + 1) * 512], in_=pt[0:64, :]), reads=[pb], writes=[kab[0]])
            p.op(p.dve, lambda e: e.tensor_copy(out=kaug[1][0:64, tg * 512:(tg + 1) * 512], in_=pt[64:128, :]), reads=[pb], writes=[kab[1]])
        for t4 in range(4):
            pt, pb = self.ps.get()
            for j in range(4):
                t = t4 * 4 + j
                for k in range(8):
                    self.mm(pt[:, j * 128:(j + 1) * 128], self.hbuf[:, k, t * 128:(t + 1) * 128], slab[:, k, 256:384],
                            k == 0, k == 7, [slb, self.hb[t4]], pb)
            p.op(p.dve, lambda e: e.tensor_copy(out=vsw[:, t4 * 4:(t4 + 1) * 4, :, 0:64], in_=pt[:].rearrange("p (j w c) -> p j w c", j=4, w=2)),
                 reads=[pb], writes=[vswb])
        slab, slb = self.wslab(wv[:, :, C_NQ + g * 256:C_NQ + (g + 1) * 256])
        for rp in range(2):
            for tg in range(NG):
                pt, pb = self.ps.get()
                for k in range(8):
                    self.mm(pt[:], slab[:, k, rp * 128:(rp + 1) * 128], self.hbuf[:, k, tg * 512:(tg + 1) * 512], k == 0, k == 7, [slb, self.hb[tg]], pb)
                p.op(p.act, lambda e: e.activation(out=qa[2 * rp][0:64, tg * 512:(tg + 1) * 512], in_=pt[0:64, :], func=AF.Copy, scale=0.125),
                     reads=[pb], writes=[qab[2 * rp]])
                p.op(p.dve, lambda e: e.tensor_scalar(out=qa[2 * rp + 1][0:64, tg * 512:(tg + 1) * 512], in0=pt[64:128, :], scalar1=0.125, scalar2=None,
                                                      op0=ALU.mult), reads=[pb], writes=[qab[2 * rp + 1]])
        if True:
            kcr, kcrb = A("nkcr", [64, 16, 128], BF16)
            gT, gTb = A("ngT", [64, 128], BF16)
            peT, peb = A("npeT", [64, 2, 32], BF16)
            bia, _biab = A("nbia", [64, 2], F32)
            p.dma(p.pool, peT[:], self.nsa_peT[:, l], writes=[peb])
            slabc, slcb = kvslab, kvslb
            kcr2, kcr2b = A("nkcr2", [64, 16, 128], BF16)
            kcrs = [kcr, kcr2]
            kcrbs = [kcrb, kcr2b]
            for tg in range(NG):
                pt, pb = self.ps.get()
                for k in range(8):
                    self.mm(pt[:], slabc[:, k, 128:256], self.hbuf[:, k, tg * 512:(tg + 1) * 512], k == 0, k == 7, [slcb, self.hb[tg]], pb)
                p.op(p.act, lambda e: e.copy(out=kcrs[0][:, :, tg * 32:(tg + 1) * 32], in_=pt[0:64, :].rearrange("p (m r) -> p r m", r=16)),
                     reads=[pb], writes=[kcrbs[0]])
                p.op(p.dve, lambda e: e.tensor_copy(out=kcrs[1][:, :, tg * 32:(tg + 1) * 32], in_=pt[64:128, :].rearrange("p (m r) -> p r m", r=16)),
                     reads=[pb], writes=[kcrbs[1]])
            for kv in range(2):
                kcr, kcrb = kcrs[kv], kcrbs[kv]
                w1, w1b = self.wslab(self.phi1[kv][l].rearrange("(j d) o -> d j o", d=64))
                w2, w2b = self.wslab(self.phi2[kv][l])
                pp, ppb = self.ps.get()
                for j in range(32):
                    jh, jr = divmod(j, 16)
                    self.mm(pp[0:64, 0:127], w1[:, j, :], kcr[:, jr, jh:jh + 127], j == 0, j == 31, [w1b, kcrb], ppb)
                for j in range(32):
                    self.mm(pp[0:64, 128:129], w1[:, j, :], peT[:, kv, j:j + 1], j == 0, j == 31, [w1b, peb], ppb)
                p.op(p.dve, lambda e: e.tensor_copy(out=bia[:, kv:kv + 1], in_=pp[0:64, 128:129]), reads=[ppb], writes=[gTb])
                p.op(p.act, lambda e: e.activation(out=gT[:, 0:127], in_=pp[0:64, 0:127], func=AF.Gelu_apprx_tanh, bias=bia[:, kv:kv + 1], scale=1.0),
                     reads=[ppb, gTb], writes=[gTb])
                pc, pcb = self.ps.get()
                if kv == 0:
                    self.mm(pc[0:64, 0:127], w2[:, 0:64], gT[:, 0:127], True, True, [w2b, gTb], pcb)
                    p.op(p.act, lambda e: e.copy(out=kcaug[0:64, 0:127], in_=pc[0:64, 0:127]), reads=[pcb], writes=[kcab])
                else:
                    self.mm(pc[0:127, 0:64], gT[:, 0:127], w2[:, 0:64], True, True, [w2b, gTb], pcb)
                    p.op(p.act, lambda e: e.copy(out=vcp[:, 0:64], in_=pc[0:127, 0:64]), reads=[pcb], writes=[vcpb])
        if b == 0 and l == 0 and g == 0:
            self.tap("kcaug", kcaug[:], kcab, [128, 128])
            self.tap("vcp", vcp[:], vcpb, [127, 97])

        def attend(r, qg, ktiles, krhs, kbuf, vfun, nv, first_branch, gcol, want_imp, kdim=128):
            acc, accb = self.ps.get_acc()
            started = [False] * 4
            last_kt = {}
            for (kt, jlo, jhi, masks, nk) in ktiles:
                for j in range(jlo, jhi + 1):
                    last_kt[j] = kt
            def scores(tile):
                (kt, jlo, jhi, masks, nk) = tile
                st_, stb = self.ps.get()
                q0, q1 = qg * 512 + jlo * 128, qg * 512 + (jhi + 1) * 128
                c0, c1 = jlo * 128, (jhi + 1) * 128
                nmm = 1 + len(masks)
                kk = kdim
                self.mm(st_[:, c0:c1], krhs(kt), qa[r][0:kk, q0:q1], True, nmm == 1, [kbuf, qab[r]], stb)
                for mi, (kind, j) in enumerate(masks):
                    lastm = (mi == len(masks) - 1)
                    if kind == "cmp":
                        self.mm(st_[:, c0:c1], self.identb[:], cmask[:, q0:q1], False, lastm, [cb, self.cstb], stb)
                    else:
                        mo = 0 if kind == "causal" else 128
                        self.mm(st_[:, j * 128:(j + 1) * 128], self.identb[:], dmask[:, mo:mo + 128], False, lastm, [cb, self.cstb], stb)
                return st_, stb

            pend, nissued = [], 0
            for ti, (kt, jlo, jhi, masks, nk) in enumerate(ktiles):
                while nissued < len(ktiles) and nissued <= ti + LOOKAHEAD:
                    pend.append(scores(ktiles[nissued]))
                    nissued += 1
                st_, stb = pend.pop(0)
                c0, c1 = jlo * 128, (jhi + 1) * 128
                P_, Pb = PT[pti[0] % NPT], PTb[pti[0] % NPT]
                pti[0] += 1
                p.op(p.act, lambda e: e.activation(out=P_[:, c0:c1], in_=st_[:, c0:c1], func=AF.Exp), reads=[stb], writes=[Pb])
                for j in range(jlo, jhi + 1):
                    self.mm(acc[:, j * nv:(j + 1) * nv], P_[0:nk, j * 128:(j + 1) * 128], vfun(kt), not any(started), last_kt[j] == kt,
                            [Pb, vswb, vcpb], accb, skip=True)
                    started[j] = True
            a3 = acc[:, 0:4 * nv].rearrange("p (j c) -> p j c", c=nv)
            sm, smb = self.small.get()
            p.op(p.dve, lambda e: e.tensor_scalar(out=sm[:, 0:4], in0=a3[:, :, 64], scalar1=1e-30, scalar2=None, op0=ALU.max), reads=[accb], writes=[smb])
            p.op(p.dve, lambda e: e.reciprocal(out=sm[:, 0:4], in_=sm[:, 0:4]), reads=[smb], writes=[smb])
            if want_imp:
                if r == 0:
                    p.op(p.dve, lambda e: e.tensor_tensor(out=impacc[:], in0=a3[:, :, 65:97], in1=sm[:, 0:4].unsqueeze(2).broadcast_to([128, 4, 32]), op=ALU.mult),
                         reads=[accb, smb], writes=[impb])
                else:
                    p.op(p.dve, lambda e: e.tensor_tensor(out=imptmp[:], in0=a3[:, :, 65:97], in1=sm[:, 0:4].unsqueeze(2).broadcast_to([128, 4, 32]), op=ALU.mult),
                         reads=[accb, smb], writes=[imptb])
                    p.op(p.dve, lambda e: e.tensor_tensor(out=impacc[:], in0=impacc[:], in1=imptmp[:], op=ALU.add), reads=[imptb, impb], writes=[impb])
            p.op(p.dve, lambda e: e.tensor_tensor(out=sm[:, 4:8], in0=sm[:, 0:4], in1=gtm[:, qg * 4:(qg + 1) * 4, gcol], op=ALU.mult), reads=[smb, cb], writes=[smb])
            if first_branch:
                p.op(p.dve, lambda e: e.tensor_tensor(out=oacc[:, :, r, :], in0=a3[:, :, 0:64], in1=sm[:, 4:8].unsqueeze(2).broadcast_to([128, 4, 64]), op=ALU.mult),
                     reads=[accb, smb], writes=[oaccb])
            else:
                p.op(p.dve, lambda e: e.tensor_tensor(out=otmp[:], in0=a3[:, :, 0:64], in1=sm[:, 4:8].unsqueeze(2).broadcast_to([128, 4, 64]), op=ALU.mult),
                     reads=[accb, smb], writes=[otmpb])
                p.op(p.dve, lambda e: e.tensor_tensor(out=oacc[:, :, r, :], in0=oacc[:, :, r, :], in1=otmp[:], op=ALU.add), reads=[otmpb, oaccb], writes=[oaccb])

        for qg in range(NG):
            for r in range(4):
                attend(r, qg, [(0, 0, 3, [("cmp", 0)], 127)], lambda kt: kcaug[:, 0:128], kcab, lambda kt: vcp[:, :], 97, True, g * 12 + r * 3 + 0, True)
            p.op(p.dve, lambda e: e.tensor_tensor(out=impacc[:], in0=impacc[:], in1=ka[:, qg * 4:(qg + 1) * 4, 0:32], op=ALU.mult), reads=[impb, cb], writes=[impb])
            p.op(p.dve, lambda e: e.tensor_tensor(out=impacc[:], in0=impacc[:], in1=ka[:, qg * 4:(qg + 1) * 4, 32:64], op=ALU.add), reads=[impb, cb], writes=[impb])
            for j in range(4):
                m8, m8b = self.small.get()
                p.op(p.dve, lambda e: e.max(out=m8[:, 0:8], in_=impacc[:, j, :]), reads=[impb], writes=[m8b])
                p.op(p.dve, lambda e: e.tensor_scalar(out=mskb[:, j, :], in0=impacc[:, j, :], scalar1=m8[:, 7:8], scalar2=1.0, op0=ALU.is_ge, op1=ALU.subtract),
                     reads=[impb, m8b], writes=[mskbb])
            for r in range(4):
                kts = []
                for kt in range(max(0, 4 * qg - 4), 4 * qg + 4):
                    jlo = max(kt - 4 * qg, 0)
                    jhi = min(kt + 4 - 4 * qg, 3)
                    masks = []
                    if kt - 4 * qg >= 0:
                        masks.append(("causal", kt - 4 * qg))
                    if 0 <= kt + 4 - 4 * qg <= 3:
                        masks.append(("band", kt + 4 - 4 * qg))
                    kts.append((kt, jlo, jhi, masks, 128))
                attend(r, qg, kts, lambda kt: kaug[1][:, kt * 128:(kt + 1) * 128], kab[1], lambda kt: vsw[:, kt, 1, :], 65, False, g * 12 + r * 3 + 2, False)
            pm, pmb = self.ps.get()
            pmv = pm[:].bitcast(BF16)
            for j in range(4):
                p.op(p.pe, lambda e: e.transpose(out=pmv[0:32, j * 128:(j + 1) * 128], in_=mskb[:, j, :], identity=self.identb[:]),
                     reads=[mskbb, self.cstb], writes=[pmb], pe_accum=(j > 0))
            p.op(p.act, lambda e: e.activation(out=mbT[:], in_=pmv[0:32, 0:512], func=AF.Copy, scale=30000.0), reads=[pmb], writes=[mbTb])
            for r in range(4):
                p.dma(p.sp, qa[r][64:96, qg * 512:(qg + 1) * 512], mbT[:], reads=[mbTb], writes=[qab[r]])
            for r in range(4):
                kts = []
                for kt in range(4 * qg + 4):
                    i = kt - 4 * qg
                    if i < 0:
                        kts.append((kt, 0, 3, [], 128))
                    else:
                        kts.append((kt, i, 3, [("causal", i)], 128))
                attend(r, qg, kts, lambda kt: kaug[0][:, kt * 128:(kt + 1) * 128], kab[0], lambda kt: vsw[:, kt, 0, :], 65, False, g * 12 + r * 3 + 1, False)
            for rp in range(2):
                pt, pb = self.ps.get()
                for j in range(4):
                    p.op(p.pe, lambda e: e.transpose(out=pt[:, j * 128:(j + 1) * 128], in_=oacc[:, j, rp * 2:rp * 2 + 2, :].rearrange("p r d -> p (r d)"),
                                                     identity=self.ident), reads=[oaccb, self.cstb], writes=[pb], pe_accum=(j > 0))
                p.op(p.act, lambda e: e.copy(out=oT[:, g * 2 + rp, qg * 512:(qg + 1) * 512], in_=pt[:]), reads=[pb], writes=[ob[qg]])


CO_ID = 0
CO_EPS = 128
CO_TRI = 129
CO_ONE = 257
CO_LNS = 258
CO_ONES = 259
NCONST = 259 + 128


def make_consts():
    c = np.zeros((128, NCONST), np.float32)
    c[:, CO_ID:CO_ID + 128] = np.eye(128, dtype=np.float32)
    c[:, CO_EPS] = EPS
    pp = np.arange(128)
    c[:, CO_TRI:CO_TRI + 128] = (pp[:, None] <= pp[None, :]).astype(np.float32)
    c[:, CO_ONE] = 1.0
    c[:, CO_LNS] = -0.5 * np.log(128.0)
    c[:, CO_ONES:CO_ONES + 128] = 1.0
    return c


def nsa_consts():
    d = {}
    pos = np.arange(S)
    d["nsa_krows"] = np.stack([64.0 * (pos // 64), 1.0 * (pos % 64), np.ones(S), np.ones(S)]).astype(np.float32)
    slopes = 2.0 ** (-(np.arange(8) + 1.0))
    R = np.stack([np.ones(S), np.ones(S), -64.0 * (pos // 64), -1.0 * (pos % 64)])
    d["nsa_qrows"] = (slopes[:, None, None] * R[None]).astype(np.float32)
    cc = np.arange(127)
    d["nsa_crows"] = np.stack([16.0 * cc, np.full(127, 15.5), np.ones(127), np.ones(127)]).astype(np.float32)
    cend = 16 * cc + 31
    cm = np.zeros((128, S), np.float32)
    cm[0:127] = np.where(cend[:, None] <= pos[None, :], 0.0, -30000.0)
    d["nsa_cmask"] = cm
    d["nsa_onehot"] = (np.arange(32)[:, None] == (pos[None, :] // 64)).astype(np.float32)
    pp = np.arange(128)
    dm = np.zeros((128, 256), np.float32)
    dm[:, 0:128] = np.where(pp[:, None] <= pp[None, :], 0.0, -30000.0)
    dm[:, 128:256] = np.where(pp[:, None] > pp[None, :], 0.0, -30000.0)
    d["nsa_dmask"] = dm
    t = pos.reshape(16, 128).T
    jt = t // 64
    jj = np.arange(32)[None, None, :]
    fut = jj > jt[:, :, None]
    forced = ((jj == 0) | (jj == jt[:, :, None]) | (jj == jt[:, :, None] - 1)) & ~fut
    keep = np.where(fut | forced, 0.0, 1.0)
    add = np.where(fut, -1e30, np.where(forced, 1e4, 0.0))
    d["nsa_ka"] = np.concatenate([keep, add], axis=2).astype(np.float32)
    cs = 16 * cc
    sel = np.arange(32)
    ovl = ((cs[:, None] <= sel[None, :] * 64 + 63) & (cs[:, None] + 31 >= sel[None, :] * 64)).astype(np.float32)
    d["nsa_ovl"] = np.concatenate([np.ones((127, 1), np.float32), ovl], axis=1)
    return d


def prep_shared(inp, L):
    f = lambda a: np.ascontiguousarray(np.asarray(a, dtype=np.float32))
    d = {}
    d["w_ada"] = f(inp["w_ada"][:L])
    d["b_adaT"] = f(np.asarray(inp["b_ada"])[:L].reshape(L, 48, 128).transpose(2, 0, 1))
    perm = np.arange(PT)
    for g in range(2):
        for i, c0 in enumerate((C_NKS, C_NKW, C_NKC, C_NVC, C_NVS, C_NVW)):
            perm[C_NKC + g * 384 + i * 64: C_NKC + g * 384 + (i + 1) * 64] = np.arange(c0 + g * 64, c0 + (g + 1) * 64)
    for h in range(4):
        for i, c0 in enumerate((C_MQ, C_MK, C_MV, C_MO)):
            perm[C_MQ + h * 512 + i * 128: C_MQ + h * 512 + (i + 1) * 128] = np.arange(c0 + h * 128, c0 + (h + 1) * 128)
    d["w_in"] = f(np.asarray(inp["w_in"], dtype=np.float32)[:L][:, :, perm])
    gn = []
    for l in range(L):
        gn.append(np.asarray(inp["g_norm1"])[l])
        gn.append(np.asarray(inp["g_norm2"])[l])
    gn.append(np.asarray(inp["g_final"]))
    d["gnT"] = f(np.stack(gn).reshape(2 * L + 1, 8, 128).transpose(2, 0, 1))
    for m in "abc":
        d["w_up_" + m] = f(inp["w_up_" + m][:L])
    d["w_out"] = f(inp["w_out"][:L])
    d["w_mlp1"] = f(inp["w_mlp1"][:L])
    d["w_mlp2"] = f(inp["w_mlp2"][:L])
    d["consts"] = make_consts()
    A = lambda k: np.asarray(inp[k], dtype=np.float32)[:L]
    ln = np.stack([A("gm_ln_g"), A("gm_ln_b")], axis=1)
    d["gm_ln"] = f(np.broadcast_to(ln[None], (128, L, 2, 512)))
    d["gm_wsT"] = f(A("gm_ws").transpose(3, 0, 1, 2))
    d["gm_bs"] = f(A("gm_bs").reshape(1, L * 4 * 128))
    mlp = np.zeros((128, L, 52), np.float32)
    mlp[:, :, 0:32] = A("ml_conv_w").reshape(L, 4, 8, 128).transpose(3, 0, 2, 1).reshape(128, L, 32)
    mlp[:, :, 32:40] = A("ml_conv_b").reshape(L, 8, 128).transpose(2, 0, 1)
    mlp[:, :, 40:48] = A("ml_gate_b")[None]
    mlp[:, :, 48:52] = A("ml_norm_g").reshape(L, 4, 128).transpose(2, 0, 1)
    d["ml_par"] = mlp
    d["nsa_peT"] = f(np.stack([A("nsa_pe_k"), A("nsa_pe_v")], axis=1).transpose(3, 0, 1, 2))
    for nm in ("nsa_phi_k1", "nsa_phi_v1", "nsa_phi_k2", "nsa_phi_v2"):
        d[nm] = f(A(nm))
    d.update(nsa_consts())
    return d


def run(inp, ncores, nseq, L, mixers="abc", taps=()):
    global LAST_RES
    k = K(nseq, L, mixers, taps)
    nc = k.build()
    shared = prep_shared(inp, L)
    x = np.asarray(inp["x"], dtype=np.float32)
    c = np.asarray(inp["c"], dtype=np.float32)
    in_maps = []
    for i in range(ncores):
        m = dict(shared)
        m["x"] = np.ascontiguousarray(x[i * nseq:(i + 1) * nseq].reshape(nseq * S, D))
        m["cT"] = np.ascontiguousarray(c[i * nseq:(i + 1) * nseq].reshape(nseq, 8, 128).transpose(2, 1, 0))
        in_maps.append(m)
    res = run_bass_kernel_spmd(nc, in_maps, core_ids=list(range(ncores)))
    out = np.concatenate([r["out"].reshape(nseq, S, D) for r in res.results], axis=0)
    return out, res, k


def kernel(**inputs):
    out, _, _ = run(inputs, 8, 4, 2)
    return out.astype(np.float32)
```

```python
import contextlib
import numpy as np
import concourse.bass as bass
import concourse.mybir as mybir
from concourse.bass_utils import run_bass_kernel_spmd

F32 = mybir.dt.float32
BF16 = mybir.dt.bfloat16
AF = mybir.ActivationFunctionType
ALU = mybir.AluOpType
AX = mybir.AxisListType

S = 2048
D = 1024
DFF = 4096
NT = 16
NG = 4
PT = 7456
EPS = 1e-6
C_GU, C_GV, C_MQ, C_MK, C_MV, C_MO, C_MI, C_MF = 0, 512, 1024, 1536, 2048, 2560, 3072, 3076
C_NQ, C_NKC, C_NVC, C_NKS, C_NVS, C_NKW, C_NVW, C_NG = 3080, 3592, 3720, 3848, 3976, 4104, 4232, 4360
C_GA, C_GB, C_GC = 4384, 5408, 6432


class Buf:
    __slots__ = ("name", "w", "r")

    def __init__(self, name=""):
        self.name = name
        self.w = None
        self.r = {}


class Eng:
    def __init__(self, prog, name, h, sem):
        self.name = name
        self.h = h
        self.sem = sem
        self.cnt = 0
        self.waited = {}


class Prog:
    def __init__(self, nc, st, n_dma_sems=40):
        self.nc = nc
        self.sems = {}
        mk = lambda n: st.enter_context(nc.semaphore(n))
        self.pe = Eng(self, "pe", nc.tensor, mk("s_pe"))
        self.act = Eng(self, "act", nc.scalar, mk("s_act"))
        self.dve = Eng(self, "dve", nc.vector, mk("s_dve"))
        self.pool = Eng(self, "pool", nc.gpsimd, mk("s_pool"))
        self.sp = Eng(self, "sp", nc.sync, mk("s_sp"))
        for e in (self.pe, self.act, self.dve, self.pool, self.sp):
            self.sems[e.name] = e.sem
        self.dsems = []
        for i in range(n_dma_sems):
            s = mk("s_d%d" % i)
            self.sems["d%d" % i] = s
            self.dsems.append(["d%d" % i, 0])
        self.dpools = {"sw": [0, list(range(0, n_dma_sems // 2))], "hw": [0, list(range(n_dma_sems // 2, n_dma_sems))]}
        self.nins = 0

    def _wait(self, eng, deps):
        need = {}
        for d in deps:
            if d is None:
                continue
            k, v = d
            if v > need.get(k, 0):
                need[k] = v
        for k, v in need.items():
            if v > eng.waited.get(k, 0):
                eng.h.wait_ge(self.sems[k], v)
                eng.waited[k] = v

    def _deps(self, eng, reads, writes, pe_accum):
        deps = []
        for b in reads:
            deps.append(b.w)
        for b in writes:
            if not ((pe_accum or eng.name == "pe") and b.w is not None and b.w[0] == "pe"):
                deps.append(b.w)
            for k, v in b.r.items():
                deps.append((k, v))
        return deps

    def _mark(self, tk, reads, writes):
        k, v = tk
        for b in reads:
            if v > b.r.get(k, 0):
                b.r[k] = v
        for b in writes:
            b.w = tk
            b.r = {}

    def op(self, eng, fn, reads=(), writes=(), pe_accum=False):
        self._wait(eng, self._deps(eng, reads, writes, pe_accum))
        ins = fn(eng.h)
        eng.cnt += 1
        ins.then_inc(eng.sem, 1)
        self.nins += 1
        self._mark((eng.name, eng.cnt), reads, writes)

    def dma(self, q, out, in_, reads=(), writes=(), **kw):
        deps = self._deps(q, reads, writes, False)
        pl = self.dpools["sw" if q is self.pool else "hw"]
        ds = self.dsems[pl[1][pl[0] % len(pl[1])]]
        pl[0] += 1
        if ds[1] > 0:
            deps.append((ds[0], 16 * ds[1]))
        self._wait(q, deps)
        q.h.dma_start(out=out, in_=in_, **kw).then_inc(self.sems[ds[0]], 16)
        ds[1] += 1
        self.nins += 1
        tk = (ds[0], 16 * ds[1])
        self._mark(tk, reads, writes)
        return tk

    def finish(self, bufs):
        deps = []
        for b in bufs:
            deps.append(b.w)
            deps.extend(b.r.items())
        for ds in self.dsems:
            if ds[1] > 0:
                deps.append((ds[0], 16 * ds[1]))
        for e in (self.pe, self.act, self.dve, self.pool):
            deps.append((e.name, e.cnt))
        self._wait(self.sp, deps)


class PsumPool:
    def __init__(self, nc, n=8, ngen=6):
        self.t = [nc.alloc_psum_tensor("ps%d" % i, [128, 512], F32) for i in range(n)]
        self.b = [Buf("ps%d" % i) for i in range(n)]
        self.i = 0
        self.n = ngen
        self.j = 0
        self.nacc = n - ngen

    def get(self):
        i = self.i
        self.i = (self.i + 1) % self.n
        return self.t[i], self.b[i]

    def get_acc(self):
        j = self.n + self.j
        self.j = (self.j + 1) % self.nacc
        return self.t[j], self.b[j]


class Rot:
    def __init__(self, nc, name, shape, dtype, n):
        self.t = [nc.alloc_sbuf_tensor("%s%d" % (name, i), shape, dtype) for i in range(n)]
        self.b = [Buf("%s%d" % (name, i)) for i in range(n)]
        self.i = 0
        self.n = n

    def get(self):
        i = self.i
        self.i = (self.i + 1) % self.n
        return self.t[i], self.b[i]


class Rot2:
    def __init__(self, tiles):
        self.t = tiles
        self.b = [Buf("r%d" % i) for i in range(len(tiles))]
        self.i = 0

    def get(self):
        i = self.i
        self.i = (self.i + 1) % len(self.t)
        return self.t[i], self.b[i]


class K:
    def __init__(self, nseq, nlayers, mixers="abc", taps=()):
        self.nseq, self.L, self.mixers, self.taps = nseq, nlayers, mixers, taps
        self.nc = nc = bass.Bass("TRN2", target_bir_lowering=False)
        self.st = contextlib.ExitStack()
        self.uid = 0
        self.out_bufs = []

    def dram_in(self, name, shape, dt=F32):
        return self.nc.dram_tensor(name, list(shape), dt, kind="ExternalInput").ap()

    def sb(self, st, name, shape, dt):
        self.uid += 1
        return st.enter_context(self.nc.sbuf_tensor("%s_%d" % (name, self.uid), list(shape), dt))

    def barrier(self):
        p = self.p
        engs = (p.pe, p.act, p.dve, p.pool, p.sp)
        deps = [(e.name, e.cnt) for e in engs if e.cnt > 0]
        hw = set(p.dpools["hw"][1])
        for i, ds in enumerate(p.dsems):
            if ds[1] > 0 and i in hw:
                deps.append((ds[0], 16 * ds[1]))
        for e in engs:
            p._wait(e, deps)

    @contextlib.contextmanager
    def scope(self):
        with contextlib.ExitStack() as s:
            yield s
            self.barrier()

    def prefetch(self, key, src_ap):
        if not hasattr(self, "_pf"):
            self._pf = {}
        self._pf[key] = self.wslab(src_ap)

    def wslab(self, src_ap, shape=None, key=None):
        if key is not None and getattr(self, "_pf", None) and key in self._pf:
            return self._pf.pop(key)
        t, b = self.wrot.get()
        shp = list(src_ap.shape)
        np_ = shp[0]
        if len(shp) == 3:
            dst = t[0:np_, 0:shp[1], 0:shp[2]] if shp[2] == 512 else \
                t[0:np_].rearrange("p a b -> p (a b)")[:, 0:shp[1] * shp[2]].rearrange("p (a b) -> p a b", b=shp[2])
        else:
            dst = t[0:np_].rearrange("p a b -> p (a b)")[:, 0:shp[1]]
        self.p.dma(self.p.pool, dst, src_ap, reads=list(getattr(self, "_wdeps", [])), writes=[b])
        return dst, b

    def mm(self, out, lhsT, rhs, start, stop, reads, wbuf, skip=False):
        self.p.op(self.p.pe, lambda e: e.matmul(out, lhsT, rhs, start=start, stop=stop, skip_group_check=skip),
                  reads=reads, writes=[wbuf], pe_accum=not start)

    def tap(self, name, ap, buf, shape):
        if name not in self.taps:
            return
        d = self.nc.dram_tensor("tap_" + name, list(shape), ap.dtype, kind="ExternalOutput").ap()
        b = Buf("tap")
        self.p.dma(self.p.sp, d, ap, reads=[buf], writes=[b])
        self.out_bufs.append(b)

    def build(self):
        nc, nseq, L = self.nc, self.nseq, self.L
        st = self.st
        self.p = p = Prog(nc, st)
        di = self.dram_in
        self.x = di("x", [nseq * S, D])
        self.cT = di("cT", [128, 8, nseq])
        self.w_ada = di("w_ada", [L, D, 6 * D])
        self.b_adaT = di("b_adaT", [128, L, 48])
        self.w_in = di("w_in", [L, D, PT])
        self.gnT = di("gnT", [128, 2 * L + 1, 8])
        self.w_up = [di("w_up_" + m, [L, 512, D]) for m in "abc"]
        self.w_out = di("w_out", [L, D, D])
        self.w_mlp1 = di("w_mlp1", [L, D, DFF])
        self.w_mlp2 = di("w_mlp2", [L, DFF, D])
        self.consts = di("consts", [128, NCONST])
        self.declare_mixer_inputs()
        self.out = nc.dram_tensor("out", [nseq * S, D], F32, kind="ExternalOutput").ap()
        self.xs = nc.dram_tensor("xs", [nseq, 128, 8, S], F32, kind="Internal").ap()
        self.wf1 = nc.dram_tensor("wf1", [L, D, DFF], BF16, kind="Internal").ap()
        self.wf2 = nc.dram_tensor("wf2", [L, DFF, D], BF16, kind="Internal").ap()
        self.wfb = [[[], []] for l in range(L)]
        self.xsb = [[Buf("xs%d_%d" % (b, g)) for g in range(NG)] for b in range(nseq)]

        sbp = lambda name, shape, dt: nc.alloc_sbuf_tensor(name, list(shape), dt)
        self.ps = PsumPool(nc, 8)
        self.wrot = Rot(nc, "wsl", [128, 8, 512], BF16, 3)
        self.f32t = Rot(nc, "f32t", [128, 512], F32, 5)
        self.bf16t = Rot(nc, "bf16t", [128, 512], BF16, 3)
        self.cst = sbp("cst", [128, NCONST], F32)
        self.cstb = Buf("cst")
        p.dma(p.sp, self.cst[:], self.consts, writes=[self.cstb])
        self.ident = self.cst[:, CO_ID:CO_ID + 128]
        self.identb = sbp("identb", [128, 128], BF16)
        self.onesb = sbp("onesb", [128, 128], BF16)
        p.op(p.dve, lambda e: e.tensor_copy(self.identb[:], self.ident), reads=[self.cstb], writes=[self.cstb])
        p.op(p.dve, lambda e: e.memset(self.onesb[:], 1.0), writes=[self.cstb])
        self.gn = sbp("gn", [128, 2 * L + 1, 8], F32)
        p.dma(p.sp, self.gn[:], self.gnT, writes=[self.cstb])
        self.hbuf = sbp("hbuf", [128, 8, S], BF16)
        self.hb = [Buf("h%d" % g) for g in range(NG)]
        self.setup_mixer_consts(sbp)

        self.phase0(sbp)
        self.tap("modT", self.modT[:], self.modb, [128, L, nseq, 48])
        self.tap("A", self.A[:], self.modb, [128, L, nseq, 2, 8])
        for b in range(nseq):
            if b == 0:
                self.x0(b)
            for l in range(L):
                wvl = self.w_in[l].rearrange("(k p) c -> p k c", p=128)
                with self.scope() as s1:
                    outs = {}
                    for m in self.mixers:
                        outs[m] = (self.sb(s1, "out" + m, [128, 4, S], BF16), [Buf("o%s%d" % (m, g)) for g in range(NG)])
                    if "a" in self.mixers:
                        self.gmlp(b, l, *outs["a"])
                    if b == 0:
                        self.convert_ffn(l)
                    if "b" in self.mixers:
                        self.mlstm(b, l, *outs["b"])
                    if "c" in self.mixers:
                        self.nsa(b, l, *outs["c"])
                    if b == 0 and l == 0:
                        for m in self.mixers:
                            self.tap("out" + m, outs[m][0][:], outs[m][1][NG - 1], [128, 4, S])
                    self.stage1(b, l, outs, s1)
                    if b == 0 and l == 0:
                        self.tap("h2", self.hbuf[:, :, 0:512], self.hb[0], [128, 8, 512])
                with self.scope() as s2:
                    self.stage2(b, l, s2)
        p.finish(self.out_bufs)
        st.close()
        return nc

    def convert_ffn(self, l):
        p = self.p
        for r0 in range(0, D, 128):
            bb = Buf("wf1")
            p.dma(p.pool, self.wf1[l][r0:r0 + 128, :], self.w_mlp1[l][r0:r0 + 128, :], writes=[bb])
            self.wfb[l][0].append(bb)
        for r0 in range(0, DFF, 256):
            bb = Buf("wf2")
            p.dma(p.pool, self.wf2[l][r0:r0 + 256, :], self.w_mlp2[l][r0:r0 + 256, :], writes=[bb])
            self.wfb[l][1].append(bb)

    def phase0(self, sbp):
        p, nseq, L = self.p, self.nseq, self.L
        self.modT = sbp("modT", [128, L, nseq, 48], F32)
        self.A = sbp("Amod", [128, L, nseq, 2, 8], F32)
        self.modb = Buf("mod")
        bad = sbp("bad", [128, L, 48], F32)
        cTf = sbp("cTf", [128, 8, nseq], F32)
        cTb = sbp("cTb", [128, 8, nseq], BF16)
        p.dma(p.sp, bad[:], self.b_adaT, writes=[self.modb])
        p.dma(p.sp, cTf[:], self.cT, writes=[self.modb])
        p.op(p.act, lambda e: e.activation(out=cTb[:], in_=cTf[:], func=AF.Silu), reads=[self.modb], writes=[self.modb])
        for l in range(L):
            wv = self.w_ada[l].rearrange("(k p) c -> p k c", p=128)
            for cg in range(12):
                slab, sbuf = self.wslab(wv[:, :, cg * 512:(cg + 1) * 512])
                for j in range(4):
                    jj = cg * 4 + j
                    pt, pb = self.ps.get()
                    for k in range(8):
                        self.mm(pt[:, 0:nseq], slab[:, k, j * 128:(j + 1) * 128], cTb[:, k, :], k == 0, k == 7,
                                [sbuf, self.modb], pb)
                    p.op(p.dve, lambda e: e.tensor_scalar(out=self.modT[:, l, :, jj], in0=pt[:, 0:nseq],
                                                          scalar1=bad[:, l, jj:jj + 1], scalar2=None, op0=ALU.add),
                         reads=[pb, self.modb], writes=[self.modb])
            for b in range(nseq):
                for w, off in ((0, 8), (1, 32)):
                    p.op(p.dve, lambda e: e.scalar_tensor_tensor(
                        out=self.A[:, l, b, w, :], in0=self.modT[:, l, b, off:off + 8], scalar=1.0,
                        in1=self.gn[:, 2 * l + w, :], op0=ALU.add, op1=ALU.mult),
                        reads=[self.modb, self.cstb], writes=[self.modb])

    def mod(self, l, b, which):
        return self.modT[:, l, b, which * 8:(which + 1) * 8]

    def norm_to_h(self, l, b, w, tg, gfinal=False):
        p = self.p
        pt, pb = self.ps.get()
        for k in range(8):
            sq, sqb = self.bf16t.get()
            p.op(p.act, lambda e: e.activation(out=sq[:], in_=self.xT[:, k, :], func=AF.Square),
                 reads=[self.xTb], writes=[sqb])
            self.mm(pt[:], self.onesb[:], sq[:], k == 0, k == 7, [sqb, self.cstb], pb)
        rs, rsb = self.rsrot.get()
        p.op(p.act, lambda e: e.activation(out=rs[:], in_=pt[:], func=AF.Sqrt, scale=1.0 / D, bias=self.epsc),
             reads=[pb, self.cstb], writes=[rsb])
        p.op(p.dve, lambda e: e.reciprocal(out=rs[:], in_=rs[:]), reads=[rsb], writes=[rsb])
        if not hasattr(self, "_t1"):
            self._t1 = 1
            self.tap("rs", rs[:], rsb, [128, 512])
            self.tap("xT0", self.xT[:], self.xTb, [128, 8, 512])
        return rs, rsb

    def norm1(self, l, b, w, tg):
        p = self.p
        rs, rsb = self.norm_to_h(l, b, w, tg)
        for k in range(8):
            tm, tmb = self.f32t.get()
            p.op(p.dve, lambda e: e.tensor_tensor(out=tm[:], in0=self.xT[:, k, :], in1=rs[:], op=ALU.mult),
                 reads=[self.xTb, rsb], writes=[tmb])
            p.op(p.act, lambda e: e.activation(out=self.hbuf[:, k, tg * 512:(tg + 1) * 512], in_=tm[:],
                                               func=AF.Identity, scale=self.A[:, l, b, w, k:k + 1],
                                               bias=self.modT[:, l, b, 24 * w + k:24 * w + k + 1]),
                 reads=[tmb, self.modb], writes=[self.hb[tg]])

    def x0(self, b):
        p = self.p
        with self.scope() as sx:
            self._x0(b, sx)

    @contextlib.contextmanager
    def use_x(self, xT, xTb):
        old = (self.xT, self.xTb)
        self.xT, self.xTb = xT, xTb
        try:
            yield
        finally:
            self.xT, self.xTb = old

    def alloc_x(self, sx):
        self.xT = self.sb(sx, "xT", [128, 8, 512], F32)
        self.xTb = Buf("xT")
        self.rsrot = Rot2([self.sb(sx, "rs", [128, 512], F32) for _ in range(2)])

    def _x0(self, b, sx):
        self.alloc_x(sx)
        xins = [self.sb(sx, "xin", [128, 1024], F32) for _ in range(2)]
        xinbs = [Buf("xin0"), Buf("xin1")]
        for tg in range(NG):
            self._x0_tg(b, tg, xins, xinbs)

    def _x0_load(self, b, tg, xins, xinbs):
        for tt in range(4):
            t = tg * 4 + tt
            self.p.dma(self.p.sp, xins[tt][:], self.x[b * S + t * 128: b * S + (t + 1) * 128, :], writes=[xinbs[tt]])

    def _x0_tg(self, b, tg, xins, xinbs, preloaded=False):
        p = self.p
        for tt in range(4):
            t = tg * 4 + tt
            xin, xinb = (xins[tt], xinbs[tt]) if preloaded else (xins[t % 2], xinbs[t % 2])
            if not preloaded:
                p.dma(p.sp, xin[:], self.x[b * S + t * 128: b * S + (t + 1) * 128, :], writes=[xinb])
            for half in range(2):
                pt, pb = self.ps.get()
                for j in range(4):
                    k = half * 4 + j
                    p.op(p.pe, lambda e: e.transpose(out=pt[:, j * 128:(j + 1) * 128], in_=xin[:, k * 128:(k + 1) * 128],
                                                     identity=self.ident),
                         reads=[xinb, self.cstb], writes=[pb], pe_accum=(j > 0))
                src = pt[:].rearrange("p (j t) -> p j t", j=4)
                dst = self.xT[:, half * 4:half * 4 + 4, tt * 128:(tt + 1) * 128]
                if half == 0:
                    p.op(p.act, lambda e: e.copy(out=dst, in_=src), reads=[pb], writes=[self.xTb])
                else:
                    p.op(p.dve, lambda e: e.tensor_copy(out=dst, in_=src), reads=[pb], writes=[self.xTb])
        p.dma(p.sp, self.xs[b][:, :, tg * 512:(tg + 1) * 512], self.xT[:], reads=[self.xTb], writes=[self.xsb[b][tg]])
        self.norm1(0, b, 0, tg)

    def stage1(self, b, l, outs, s1):
        p = self.p
        mgh = self.sb(s1, "mgh", [128, 8, S], BF16)
        mghb = [Buf("mgh%d" % g) for g in range(NG)]
        with self.scope() as smg:
            self._merge(b, l, outs, smg, mgh, mghb)
        self.alloc_x(s1)
        xTs = [(self.xT, self.xTb), (self.sb(s1, "xTb2", [128, 8, 512], F32), Buf("xT2"))]
        wo = self.w_out[l].rearrange("(k p) n -> p k n", p=128)
        wsl = [self.wslab(wo[:, :, half * 512:(half + 1) * 512]) for half in range(2)]

        def wres(tg):
            tsl = slice(tg * 512, (tg + 1) * 512)
            xT, xTb = xTs[tg % 2]
            p.dma(p.sp, xT[:], self.xs[b][:, :, tsl], reads=[self.xsb[b][tg]], writes=[xTb])
            for half in range(2):
                ws_, wsb = wsl[half]
                for j in range(4):
                    n = half * 4 + j
                    py, pyb = self.ps.get()
                    for k in range(8):
                        self.mm(py[:], ws_[:, k, j * 128:(j + 1) * 128], mgh[:, k, tsl], k == 0, k == 7, [wsb, mghb[tg]], pyb)
                    p.op(p.dve, lambda e: e.scalar_tensor_tensor(out=xT[:, n, :], in0=py[:], scalar=self.modT[:, l, b, 16 + n:17 + n],
                                                                 in1=xT[:, n, :], op0=ALU.mult, op1=ALU.add),
                         reads=[pyb, self.modb, xTb], writes=[xTb])

        wres(0)
        for tg in range(NG):
            tsl = slice(tg * 512, (tg + 1) * 512)
            if tg + 1 < NG:
                wres(tg + 1)
            xT, xTb = xTs[tg % 2]
            p.dma(p.sp, self.xs[b][:, :, tsl], xT[:], reads=[xTb], writes=[self.xsb[b][tg]])
            with self.use_x(xT, xTb):
                self.norm1(l, b, 1, tg)
        w1v = self.wf1[l].rearrange("(k p) c -> p k c", p=128)
        self._wdeps = self.wfb[l][0]
        for fs in range(2):
            self.prefetch(("w1", b, l, 0, fs), w1v[:, :, fs * 512:(fs + 1) * 512])
        self._wdeps = []

    def _merge(self, b, l, outs, smg, mgh, mghb):
        p = self.p
        mg = self.sb(smg, "mg", [128, 4, S], F32)
        mgb = [Buf("mg%d" % g) for g in range(NG)]
        wv = self.w_in[l].rearrange("(k p) c -> p k c", p=128)
        active = [(mi, m) for mi, m in enumerate("abc") if m in self.mixers]
        if not active:
            p.op(p.dve, lambda e: e.memset(mgh[:], 0.0), writes=mghb)
        for half in range(2):
            if not active:
                break
            first = True
            for mi, m in active:
                oT, ob = outs[m]
                gcol = (C_GA, C_GB, C_GC)[mi]
                kk = (mi == active[0][0] and half == 0)
                gs, gsb = self.wslab(wv[:, :, gcol + half * 512: gcol + (half + 1) * 512], key=("mg_gs", b, l) if kk else None)
                wup, wupb = self.wslab(self.w_up[mi][l].rearrange("(k p) n -> p k n", p=128)[:, :, half * 512:(half + 1) * 512],
                                       key=("mg_up", b, l) if kk else None)
                for tg in range(NG):
                    tsl = slice(tg * 512, (tg + 1) * 512)
                    for j in range(4):
                        py, pyb = self.ps.get()
                        for k in range(4):
                            self.mm(py[:], wup[:, k, j * 128:(j + 1) * 128], oT[:, k, tsl], k == 0, k == 3, [wupb, ob[tg]], pyb)
                        pg, pgb = self.ps.get()
                        for k in range(8):
                            self.mm(pg[:], gs[:, k, j * 128:(j + 1) * 128], self.hbuf[:, k, tsl], k == 0, k == 7, [gsb, self.hb[tg]], pgb)
                        sg, sgb = self.f32t.get()
                        p.op(p.act, lambda e: e.activation(out=sg[:], in_=pg[:], func=AF.Sigmoid), reads=[pgb], writes=[sgb])
                        if first:
                            p.op(p.dve, lambda e: e.tensor_tensor(out=mg[:, j, tsl], in0=py[:], in1=sg[:], op=ALU.mult),
                                 reads=[pyb, sgb], writes=[mgb[tg]])
                        else:
                            p.op(p.dve, lambda e: e.tensor_tensor(out=sg[:], in0=py[:], in1=sg[:], op=ALU.mult),
                                 reads=[pyb, sgb], writes=[sgb])
                            p.op(p.dve, lambda e: e.tensor_tensor(out=mg[:, j, tsl], in0=mg[:, j, tsl], in1=sg[:], op=ALU.add),
                                 reads=[sgb, mgb[tg]], writes=[mgb[tg]])
                first = False
            for tg in range(NG):
                tsl = slice(tg * 512, (tg + 1) * 512)
                p.op(p.act, lambda e: e.copy(out=mgh[:, half * 4:(half + 1) * 4, tsl], in_=mg[:, :, tsl]), reads=[mgb[tg]], writes=[mghb[tg]])

    def stage2(self, b, l, s2):
        p = self.p
        self.alloc_x(s2)
        aT = self.sb(s2, "aT", [128, 32, 512], BF16)
        aTb = Buf("aT")
        self.xos = [self.sb(s2, "xo", [128, 1024], F32) for _ in range(2)]
        self.xobs = [Buf("xo0"), Buf("xo1")]
        prefetch_next = (l == self.L - 1) and (b + 1 < self.nseq)
        if prefetch_next:
            nxT = self.sb(s2, "nxT", [128, 8, 512], F32)
            nxTb = Buf("nxT")
            nxins = [self.sb(s2, "nxin", [128, 1024], F32) for _ in range(4)]
            nxinbs = [Buf("nxin%d" % i) for i in range(4)]
        last = (l == self.L - 1)
        w1v = self.wf1[l].rearrange("(k p) c -> p k c", p=128)
        w2v = self.wf2[l].rearrange("(f p) n -> p f n", p=128)
        self._wdeps = self.wfb[l][0]

        def w1_phase(tg):
            tsl = slice(tg * 512, (tg + 1) * 512)
            for fs in range(8):
                self._wdeps = self.wfb[l][0]
                slab, slb = self.wslab(w1v[:, :, fs * 512:(fs + 1) * 512], key=("w1", b, l, tg, fs))
                self._wdeps = []
                for j in range(4):
                    fc = fs * 4 + j
                    pt, pb = self.ps.get()
                    for k in range(8):
                        self.mm(pt[:], slab[:, k, j * 128:(j + 1) * 128], self.hbuf[:, k, tsl], k == 0, k == 7, [slb, self.hb[tg]], pb)
                    r, rb = self.f32t.get()
                    p.op(p.act, lambda e: e.activation(out=r[:], in_=pt[:], func=AF.Relu), reads=[pb], writes=[rb])
                    p.op(p.dve, lambda e: e.tensor_tensor(out=aT[:, fc, :], in0=r[:], in1=r[:], op=ALU.mult),
                         reads=[rb], writes=[aTb])

        p.dma(p.sp, self.xT[:], self.xs[b][:, :, 0:512], reads=[self.xsb[b][0]], writes=[self.xTb])
        w1_phase(0)
        for tg in range(NG):
            tsl = slice(tg * 512, (tg + 1) * 512)
            if prefetch_next:
                self._x0_load(b + 1, tg, nxins, nxinbs)
            for fs in range(8):
                self._wdeps = self.wfb[l][1]
                slab, slb = self.wslab(w2v[:, fs * 4:(fs + 1) * 4, :])
                self._wdeps = []
                if fs < 7:
                    for j in range(4):
                        fc = fs * 4 + j
                        for n in range(8):
                            self.mm(self.ps.t[n][:], slab[:, j, n * 128:(n + 1) * 128], aT[:, fc, :], fc == 0, fc == 31,
                                    [slb, aTb], self.ps.b[n])
                else:
                    for n in range(8):
                        for j in range(4):
                            fc = fs * 4 + j
                            self.mm(self.ps.t[n][:], slab[:, j, n * 128:(n + 1) * 128], aT[:, fc, :], fc == 0, fc == 31,
                                    [slb, aTb], self.ps.b[n])
                        p.op(p.dve, lambda e: e.scalar_tensor_tensor(out=self.xT[:, n, :], in0=self.ps.t[n][:], scalar=self.modT[:, l, b, 40 + n:41 + n],
                                                                     in1=self.xT[:, n, :], op0=ALU.mult, op1=ALU.add),
                             reads=[self.ps.b[n], self.modb, self.xTb], writes=[self.xTb])
            if tg + 1 < NG:
                w1_phase(tg + 1)
            if not last:
                p.dma(p.sp, self.xs[b][:, :, tsl], self.xT[:], reads=[self.xTb], writes=[self.xsb[b][tg]])
                self.norm1(l + 1, b, 0, tg)
            else:
                self.final(b, tg)
            if tg + 1 < NG:
                p.dma(p.sp, self.xT[:], self.xs[b][:, :, (tg + 1) * 512:(tg + 2) * 512], reads=[self.xsb[b][tg + 1]], writes=[self.xTb])
            if prefetch_next:
                with self.use_x(nxT, nxTb):
                    self._x0_tg(b + 1, tg, nxins, nxinbs, preloaded=True)
        nb, nl = (b, l + 1) if l + 1 < self.L else (b + 1, 0)
        if nb < self.nseq and "a" in self.mixers:
            wvn = self.w_in[nl].rearrange("(k p) c -> p k c", p=128)
            self.prefetch(("gu", nb, nl), wvn[:, :, C_GU:C_GU + 512])
            self.prefetch(("gv", nb, nl), wvn[:, :, C_GV:C_GV + 512])

    def final(self, b, tg):
        p = self.p
        rs, rsb = self.norm_to_h(0, b, 0, tg)
        for k in range(8):
            p.op(p.dve, lambda e: e.scalar_tensor_tensor(out=self.xT[:, k, :], in0=self.xT[:, k, :], scalar=self.gn[:, 2 * self.L, k:k + 1],
                                                         in1=rs[:], op0=ALU.mult, op1=ALU.mult),
                 reads=[self.xTb, rsb, self.cstb], writes=[self.xTb])
        for tt in range(4):
            t = tg * 4 + tt
            xo, xob = self.xos[t % 2], self.xobs[t % 2]
            for half in range(2):
                pt, pb = self.ps.get()
                for j in range(4):
                    k = half * 4 + j
                    p.op(p.pe, lambda e: e.transpose(out=pt[:, j * 128:(j + 1) * 128], in_=self.xT[:, k, tt * 128:(tt + 1) * 128],
                                                     identity=self.ident),
                         reads=[self.xTb, self.cstb], writes=[pb], pe_accum=(j > 0))
                if half == 0:
                    p.op(p.act, lambda e: e.copy(out=xo[:, 0:512], in_=pt[:]), reads=[pb], writes=[xob])
                else:
                    p.op(p.dve, lambda e: e.tensor_copy(out=xo[:, 512:1024], in_=pt[:]), reads=[pb], writes=[xob])
            ob = Buf("o")
            p.dma(p.sp, self.out[b * S + t * 128: b * S + (t + 1) * 128, :], xo[:], reads=[xob], writes=[ob])
            self.out_bufs.append(ob)

    def declare_mixer_inputs(self):
        di, L = self.dram_in, self.L
        self.gm_ln = di("gm_ln", [128, L, 2, 512])
        self.gm_wsT = di("gm_wsT", [128, L, 4, 128])
        self.gm_bs = di("gm_bs", [1, L * 4 * 128])
        self.ml_par = di("ml_par", [128, L, 8 * 4 + 8 + 8 + 4])
        self.nsa_peT = di("nsa_peT", [64, L, 2, 32])
        self.phi1 = [di("nsa_phi_k1", [L, 2048, 64]), di("nsa_phi_v1", [L, 2048, 64])]
        self.phi2 = [di("nsa_phi_k2", [L, 64, 64]), di("nsa_phi_v2", [L, 64, 64])]
        self.nsa_krows = di("nsa_krows", [4, 2048])
        self.nsa_qrows = di("nsa_qrows", [8, 4, 2048])
        self.nsa_crows = di("nsa_crows", [4, 127])
        self.nsa_cmask = di("nsa_cmask", [128, 2048])
        self.nsa_onehot = di("nsa_onehot", [32, S])
        self.nsa_dmask = di("nsa_dmask", [128, 256])
        self.nsa_ka = di("nsa_ka", [128, 16, 64])
        self.nsa_ovl = di("nsa_ovl", [127, 33])

    def setup_mixer_consts(self, sbp):
        p, L = self.p, self.L
        self.epsc = self.cst[:, CO_EPS:CO_EPS + 1]
        self.tri = self.cst[:, CO_TRI:CO_TRI + 128]
        self.onesf = sbp("onesf", [1, 128], F32)
        p.op(p.dve, lambda e: e.memset(self.onesf[:], 1.0), writes=[self.cstb])
        self.small = Rot(self.nc, "small", [128, 16], F32, 8)
        self.mlp = sbp("mlp", [128, L, 52], F32)
        p.dma(p.sp, self.mlp[:], self.ml_par, writes=[self.cstb])
        self.onec = self.cst[:, CO_ONE:CO_ONE + 1]
        self.lnsc = self.cst[:, CO_LNS:CO_LNS + 1]
        self.onesF = self.cst[:, CO_ONES:CO_ONES + 128]

    def gmlp(self, b, l, oT, ob):
        p = self.p
        wv = self.w_in[l].rearrange("(k p) c -> p k c", p=128)
        with self.scope() as sc:
            uT = [self.sb(sc, "uT", [128, 4, 512], BF16) for _ in range(2)]
            uTb = [Buf("uT0"), Buf("uT1")]
            vt = [self.sb(sc, "gv", [128, 512], F32) for _ in range(8)]
            vtb = [Buf("gv%d" % i) for i in range(8)]
            vh = [self.sb(sc, "gvh", [128, 512], BF16) for _ in range(2)]
            vhb = [Buf("gvh0"), Buf("gvh1")]
            gcb = Buf("gmc")
            gmln = self.sb(sc, "gmln", [128, 2, 512], F32)
            p.dma(p.sp, gmln[:], self.gm_ln[:, l], writes=[gcb])
            wsf = self.sb(sc, "gmwsf", [128, 4, 128], F32)
            p.dma(p.sp, wsf[:], self.gm_wsT[:, l], writes=[gcb])
            gmws = self.sb(sc, "gmws", [128, 4, 128], BF16)
            p.op(p.dve, lambda e: e.tensor_tensor(out=gmws[:], in0=wsf[:], in1=self.tri.unsqueeze(1).broadcast_to([128, 4, 128]), op=ALU.mult),
                 reads=[gcb, self.cstb], writes=[gcb])
            gmbs = self.sb(sc, "gmbs", [1, 512], F32)
            p.dma(p.sp, gmbs[:], self.gm_bs[:, l * 512:(l + 1) * 512], writes=[gcb])
            slab_u, slb_u = self.wslab(wv[:, :, C_GU:C_GU + 512], key=("gu", b, l))
            slab_v, slb_v = self.wslab(wv[:, :, C_GV:C_GV + 512], key=("gv", b, l))
            mvs = {}
            mvt = [self.sb(sc, "gmv", [128, 16], F32) for _ in range(2)]
            mvtb = [Buf("gmv0"), Buf("gmv1")]

            def part1(tg):
                tsl = slice(tg * 512, (tg + 1) * 512)
                u, ub = uT[tg % 2], uTb[tg % 2]
                for g in range(4):
                    pt, pb = self.ps.get()
                    for k in range(8):
                        self.mm(pt[:], slab_u[:, k, g * 128:(g + 1) * 128], self.hbuf[:, k, tsl], k == 0, k == 7, [slb_u, self.hb[tg]], pb)
                    p.op(p.act, lambda e: e.activation(out=u[:, g, :], in_=pt[:], func=AF.Gelu_apprx_tanh), reads=[pb], writes=[ub])
                mv4, mv4b = mvt[tg % 2], mvtb[tg % 2]
                mvs[tg] = (mv4, mv4b)
                for tt in range(4):
                    t = tg * 4 + tt
                    tok = slice(t * 128, (t + 1) * 128)
                    gv, gvb = vt[t % 8], vtb[t % 8]
                    pt, pb = self.ps.get()
                    for k in range(8):
                        self.mm(pt[:], self.hbuf[:, k, tok], slab_v[:, k, :], k == 0, k == 7, [slb_v, self.hb[tg]], pb)
                    p.op(p.act, lambda e: e.activation(out=gv[:], in_=pt[:], func=AF.Gelu_apprx_tanh), reads=[pb], writes=[gvb])
                    sm, smb = self.small.get()
                    p.op(p.dve, lambda e: e.bn_stats(out=sm[:, 0:6], in_=gv[:]), reads=[gvb], writes=[smb])
                    p.op(p.dve, lambda e: e.bn_aggr(out=mv4[:, 2 * tt:2 * tt + 2], in_=sm[:, 0:6]), reads=[smb], writes=[mv4b])

            def part2(tg):
                u, ub = uT[tg % 2], uTb[tg % 2]
                mv4, mv4b = mvs[tg]
                mv3 = mv4[:, 0:8].rearrange("p (t c) -> p t c", c=2)
                p.op(p.act, lambda e: e.activation(out=mv4[:, 8:12], in_=mv3[:, :, 1], func=AF.Sqrt, bias=self.epsc, scale=1.0),
                     reads=[mv4b, self.cstb], writes=[mv4b])
                p.op(p.dve, lambda e: e.reciprocal(out=mv4[:, 12:16], in_=mv4[:, 8:12]), reads=[mv4b], writes=[mv4b])
                for tt in range(4):
                    t = tg * 4 + tt
                    tok = slice(t * 128, (t + 1) * 128)
                    gv, gvb = vt[t % 8], vtb[t % 8]
                    v, vb = vh[t % 2], vhb[t % 2]
                    p.op(p.dve, lambda e: e.tensor_scalar(out=gv[:], in0=gv[:], scalar1=mv4[:, 2 * tt:2 * tt + 1], scalar2=mv4[:, 12 + tt:13 + tt],
                                                          op0=ALU.subtract, op1=ALU.mult), reads=[gvb, mv4b], writes=[gvb])
                    p.op(p.dve, lambda e: e.tensor_tensor(out=gv[:], in0=gv[:], in1=gmln[:, 0, :], op=ALU.mult),
                         reads=[gvb, gcb], writes=[gvb])
                    p.op(p.dve, lambda e: e.tensor_tensor(out=v[:], in0=gv[:], in1=gmln[:, 1, :], op=ALU.add),
                         reads=[gvb, gcb], writes=[vb])
                    pt, pb = self.ps.get()
                    for g in range(4):
                        self.mm(pt[:, g * 128:(g + 1) * 128], v[:, g * 128:(g + 1) * 128], gmws[:, g, :], True, False,
                                [vb, gcb], pb)
                        o = g * 128
                        self.mm(pt[:, g * 128:(g + 1) * 128], self.onesf[0:1, :], gmbs[0:1, o:o + 128], False, True,
                                [self.cstb, gcb], pb)
                    p.op(p.dve, lambda e: e.tensor_tensor(out=oT[:, :, tok], in0=pt[:].rearrange("p (g t) -> p g t", g=4),
                                                          in1=u[:, :, tt * 128:(tt + 1) * 128], op=ALU.mult),
                         reads=[pb, ub], writes=[ob[tg]])

            part1(0)
            for tg in range(NG):
                if tg + 1 < NG:
                    part1(tg + 1)
                part2(tg)
            if "b" in self.mixers:
                self.prefetch(("mi", b, l), wv[:, :, C_MI:C_MI + 8])
                self.prefetch(("mh0", b, l), wv[:, :, C_MQ:C_MQ + 512])

    def mlstm(self, b, l, oT, ob):
        p = self.p
        wv = self.w_in[l].rearrange("(k p) c -> p k c", p=128)
        ALLH = [self.hb[g] for g in range(NG)]
        with self.scope() as sc:
            gt = self.sb(sc, "mlg", [128, 16, 8], F32)
            wk = self.sb(sc, "mlwk", [128, 6, 64], F32)
            row = self.sb(sc, "mlrow", [64, 256], F32)
            GB = self.sb(sc, "mlGB", [128, 128], F32)
            gb = Buf("mlgates")
            slab, slb = self.wslab(wv[:, :, C_MI:C_MI + 8], key=("mi", b, l))
            pg, pgb = self.ps.get()
            for t in range(NT):
                for k in range(8):
                    self.mm(pg[:, t * 8:(t + 1) * 8], self.hbuf[:, k, t * 128:(t + 1) * 128], slab[:, k, 0:8], k == 0, k == 7,
                            [slb, self.hb[t // 4]], pgb)
            p.op(p.dve, lambda e: e.tensor_tensor(out=gt[:], in0=pg[:, 0:128].rearrange("p (t c) -> p t c", c=8),
                                                  in1=self.mlp[:, l, 40:48].unsqueeze(1).broadcast_to([128, 16, 8]), op=ALU.add),
                 reads=[pgb, self.cstb], writes=[gb])
            v3 = lambda i: wk[:, i, :].rearrange("p (t h) -> p t h", h=4)
            p.op(p.act, lambda e: e.activation(out=v3(0), in_=gt[:, :, 4:8], func=AF.Exp, scale=-1.0), reads=[gb], writes=[gb])
            p.op(p.act, lambda e: e.activation(out=v3(0), in_=v3(0), func=AF.Ln, bias=self.onec, scale=1.0), reads=[gb, self.cstb], writes=[gb])
            pw, pwb = self.ps.get()
            self.mm(pw[:, 0:64], self.tri, wk[:, 0, :], True, True, [gb, self.cstb], pwb)
            self.mm(pw[:, 64:128], self.onesF, wk[:, 0, :], True, True, [gb, self.cstb], pwb)
            p.op(p.dve, lambda e: e.tensor_copy(out=wk[:, 1, :], in_=pw[:, 64:128]), reads=[pwb], writes=[gb])
            for h in range(4):
                p.op(p.dve, lambda e: e.tensor_tensor_scan(out=v3(5)[:, :, h], data0=self.onesF[:, 0:16], data1=v3(1)[:, :, h],
                                                           initial=0.0, op0=ALU.mult, op1=ALU.add), reads=[gb, self.cstb], writes=[gb])
            p.op(p.dve, lambda e: e.tensor_tensor(out=wk[:, 5, :], in0=wk[:, 5, :], in1=wk[:, 1, :], op=ALU.subtract), reads=[gb], writes=[gb])
            p.op(p.dve, lambda e: e.tensor_tensor(out=wk[:, 1, :], in0=wk[:, 5, :], in1=pw[:, 0:64], op=ALU.add), reads=[gb, pwb], writes=[gb])
            p.op(p.dve, lambda e: e.tensor_tensor(out=v3(2), in0=gt[:, :, 0:4], in1=v3(1), op=ALU.add), reads=[gb], writes=[gb])
            pt, pb = self.ps.get()
            p.op(p.pe, lambda e: e.transpose(out=pt[0:64, 0:128], in_=wk[:, 2, :], identity=self.ident), reads=[gb, self.cstb], writes=[pb])
            p.op(p.dve, lambda e: e.tensor_reduce(out=row[:, 0:1], in_=pt[0:64, 0:128], axis=AX.X, op=ALU.max), reads=[pb], writes=[gb])
            pt2, pb2 = self.ps.get()
            p.op(p.pe, lambda e: e.transpose(out=pt2[0:1, 0:64], in_=row[:, 0:1], identity=self.ident[0:64, 0:64]), reads=[gb, self.cstb], writes=[pb2])
            p.op(p.dve, lambda e: e.tensor_copy(out=row[0:1, 64:128], in_=pt2[0:1, 0:64]), reads=[pb2], writes=[gb])
            r3 = lambda a: row[0:1, a:a + 64].rearrange("p (t h) -> p t h", h=4)
            for h in range(4):
                p.op(p.dve, lambda e: e.tensor_tensor_scan(out=r3(128)[:, :, h], data0=r3(64)[:, :, h], data1=r3(64)[:, :, h],
                                                           initial=0.0, op0=ALU.max, op1=ALU.max), reads=[gb], writes=[gb])
            p.op(p.dve, lambda e: e.memset(row[0:1, 192:196], 0.0), writes=[gb])
            p.op(p.dve, lambda e: e.tensor_copy(out=row[0:1, 196:256], in_=row[0:1, 128:188]), reads=[gb], writes=[gb])
            p.op(p.dve, lambda e: e.tensor_tensor(out=row[0:1, 192:256], in0=row[0:1, 192:256], in1=row[0:1, 128:192], op=ALU.subtract), reads=[gb], writes=[gb])
            p.op(p.act, lambda e: e.activation(out=row[0:1, 192:256], in_=row[0:1, 192:256], func=AF.Exp), reads=[gb], writes=[gb])
            pt3, pb3 = self.ps.get()
            self.mm(pt3[:, 0:128], self.onesf[0:1, :], row[0:1, 128:256], True, True, [gb, self.cstb], pb3)
            p.op(p.dve, lambda e: e.tensor_copy(out=GB[:], in_=pt3[:, 0:128]), reads=[pb3], writes=[gb])
            p.op(p.dve, lambda e: e.tensor_tensor(out=wk[:, 3, :], in0=wk[:, 2, :], in1=GB[:, 0:64], op=ALU.subtract), reads=[gb], writes=[gb])
            p.op(p.act, lambda e: e.activation(out=wk[:, 3, :], in_=wk[:, 3, :], func=AF.Exp, bias=self.lnsc, scale=1.0), reads=[gb, self.cstb], writes=[gb])
            p.op(p.dve, lambda e: e.tensor_tensor(out=wk[:, 4, :], in0=wk[:, 1, :], in1=GB[:, 0:64], op=ALU.subtract), reads=[gb], writes=[gb])
            p.op(p.act, lambda e: e.activation(out=wk[:, 4, :], in_=wk[:, 4, :], func=AF.Exp), reads=[gb], writes=[gb])
            ev, clv = v3(3), v3(4)
            rv = GB[:, 64:128].rearrange("p (t h) -> p t h", h=4)
            self.tap("mlwk", wk[:], gb, [128, 6, 64])
            self.tap("mlGB", GB[:], gb, [128, 128])
            NSET = 2
            sets = []
            for i in range(NSET):
                d = {}
                d["qk"] = [self.sb(sc, "mlqk", [128, S], BF16) for _ in range(2)]
                d["qkb"] = [Buf("q%d" % i), Buf("k%d" % i)]
                d["vp"] = self.sb(sc, "mlv", [128, NT, 129], BF16)
                d["vpb"] = Buf("v%d" % i)
                d["kt"] = self.sb(sc, "mlkt", [128, NT, 128], BF16)
                d["ktb"] = Buf("kt%d" % i)
                d["num"] = self.sb(sc, "mlnum", [128, NT, 129], F32)
                d["numb"] = Buf("num%d" % i)
                d["D"] = self.sb(sc, "mlD", [128, NT - 1, 129], BF16)
                d["Db"] = Buf("D%d" % i)
                p.op(p.dve, lambda e: e.memset(d["vp"][:, :, 128:129], 1.0), writes=[d["vpb"]])
                sets.append(d)
            pre = [self.sb(sc, "mlpre", [128, 515], BF16) for _ in range(2)]
            dg = self.sb(sc, "mldg", [128, 8, 4, 128], BF16)
            dgb = Buf("dg")
            for c8_ in range(8):
                for j_ in range(4):
                    p.op(p.dve, lambda e: e.tensor_scalar(out=dg[:, c8_, j_, :], in0=self.identb[:], scalar1=self.mlp[:, l, c8_ * 4 + j_:c8_ * 4 + j_ + 1],
                                                          scalar2=None, op0=ALU.mult), reads=[self.cstb], writes=[dgb])
            preb = [Buf("pre0"), Buf("pre1")]
            sq4 = [self.sb(sc, "mlsq4", [128, 4, 128], F32) for _ in range(2)]
            sq4b = [Buf("sq40"), Buf("sq41")]
            NSTM = 4
            stm = [self.sb(sc, "mlstm", [128, 128], BF16) for _ in range(NSTM)]
            stmb = [Buf("stm%d" % i) for i in range(NSTM)]
            sgo = [self.sb(sc, "mlsgo", [128, 512], BF16) for _ in range(2)]
            sgob = [Buf("sgo0"), Buf("sgo1")]
            prei = [0]
            hslabs = {}

            def stage_a(h):
                d = sets[h % NSET]
                hslab, hslb = self.wslab(wv[:, :, C_MQ + h * 512:C_MQ + (h + 1) * 512], key=("mh%d" % h, b, l))
                hslabs[h] = (hslab, hslb)
                for w in range(2):
                    slab, slb = hslab[:, :, w * 128:(w + 1) * 128], hslb
                    c8 = w * 4 + h
                    prev = None
                    for tg in range(NG):
                        st_, stb_ = pre[prei[0] % 2], preb[prei[0] % 2]
                        prei[0] += 1
                        pt, pb = self.ps.get()
                        for k in range(8):
                            self.mm(pt[:], slab[:, k, :], self.hbuf[:, k, tg * 512:(tg + 1) * 512], k == 0, k == 7, [slb, self.hb[tg]], pb)
                        if prev is None:
                            p.op(p.dve, lambda e: e.memset(st_[:, 0:3], 0.0), writes=[stb_])
                        else:
                            p.op(p.act, lambda e: e.copy(out=st_[:, 0:3], in_=prev[0][:, 512:515]), reads=[prev[1]], writes=[stb_])
                        p.op(p.act, lambda e: e.copy(out=st_[:, 3:515], in_=pt[:]), reads=[pb], writes=[stb_])
                        pc_, pcb_ = self.ps.get()
                        for j in range(4):
                            self.mm(pc_[:], dg[:, c8, j, :], st_[:, j:j + 512], j == 0, j == 3, [dgb, stb_], pcb_)
                        p.op(p.act, lambda e: e.activation(out=d["qk"][w][:, tg * 512:(tg + 1) * 512], in_=pc_[:], func=AF.Silu,
                                                           bias=self.mlp[:, l, 32 + c8:33 + c8], scale=1.0), reads=[pcb_, self.cstb], writes=[d["qkb"][w]])
                        prev = (st_, stb_)
                kT = d["qk"][1]
                slab, slb = hslab[:, :, 256:384], hslb
                for t4 in range(4):
                    pt, pb = self.ps.get()
                    for j in range(4):
                        t = t4 * 4 + j
                        for k in range(8):
                            self.mm(pt[:, j * 128:(j + 1) * 128], self.hbuf[:, k, t * 128:(t + 1) * 128], slab[:, k, :], k == 0, k == 7,
                                    [slb, self.hb[t4]], pb)
                    p.op(p.act, lambda e: e.copy(out=d["vp"][:, t4 * 4:(t4 + 1) * 4, 0:128], in_=pt[:].rearrange("p (j c) -> p j c", j=4)),
                         reads=[pb], writes=[d["vpb"]])
                for t4 in range(4):
                    pt, pb = self.ps.get()
                    ptb = pt[:].bitcast(BF16)
                    for j in range(4):
                        t = t4 * 4 + j
                        p.op(p.pe, lambda e: e.transpose(out=ptb[:, j * 128:(j + 1) * 128], in_=kT[:, t * 128:(t + 1) * 128], identity=self.identb[:]),
                             reads=[d["qkb"][1], self.cstb], writes=[pb], pe_accum=(j > 0))
                    for j in range(4):
                        t = t4 * 4 + j
                        p.op(p.dve, lambda e: e.tensor_scalar(out=d["kt"][:, t, :], in0=ptb[:, j * 128:(j + 1) * 128], scalar1=ev[:, t, h:h + 1], scalar2=None,
                                                              op0=ALU.mult), reads=[pb, gb], writes=[d["ktb"]])

            def unpack(h):
                d = sets[h % NSET]
                return d["qk"][0], d["qk"][1], d["qkb"], d["vp"], d["vpb"], d["kt"], d["ktb"], d["num"], d["numb"], d["D"], d["Db"]

            def stage_b1(h):
                qT, kT, qkb, vp, vpb, kt, ktb, numall, numb, Dall, Dallb = unpack(h)
                for c3 in range(5):
                    pd, pdb = self.ps.get()
                    for j in range(3):
                        c = c3 * 3 + j
                        self.mm(pd[:, j * 129:(j + 1) * 129], kt[:, c, :], vp[:, c, :], True, True, [ktb, vpb], pdb)
                    p.op(p.act, lambda e: e.copy(out=numall[:, c3 * 3:(c3 + 1) * 3, :], in_=pd[:, 0:387].rearrange("p (j c) -> p j c", j=3)),
                         reads=[pdb], writes=[numb])
                for c in range(1, NT - 1):
                    p.op(p.dve, lambda e: e.scalar_tensor_tensor(out=numall[:, c, :], in0=numall[:, c - 1, :], scalar=rv[:, c, h:h + 1], in1=numall[:, c, :],
                                                                 op0=ALU.mult, op1=ALU.add), reads=[numb, gb], writes=[numb])
                p.op(p.dve, lambda e: e.tensor_tensor(out=Dall[:], in0=numall[:, 0:NT - 1, :],
                                                       in1=rv[:, 1:NT, h].unsqueeze(2).broadcast_to([128, NT - 1, 129]), op=ALU.mult),
                     reads=[numb, gb], writes=[Dallb])

            def stage_b2(h):
                qT, kT, qkb, vp, vpb, kt, ktb, numall, numb, Dall, Dallb = unpack(h)

                def scores(c):
                    tok = slice(c * 128, (c + 1) * 128)
                    ps_, psb = self.ps.get()
                    self.mm(ps_[:, 0:128], kT[:, tok], qT[:, tok], True, True, [qkb[0], qkb[1]], psb)
                    p.op(p.dve, lambda e: e.scalar_tensor_tensor(out=stm[c % NSTM][:], in0=ps_[:, 0:128], scalar=ev[:, c, h:h + 1], in1=self.tri,
                                                                 op0=ALU.mult, op1=ALU.mult), reads=[psb, gb, self.cstb], writes=[stmb[c % NSTM]])
                nis = 0
                for c in range(NT):
                    while nis < NT and nis <= c + 2:
                        scores(nis)
                        nis += 1
                    tok = slice(c * 128, (c + 1) * 128)
                    pn, pnb = self.ps.get()
                    if c > 0:
                        self.mm(pn[:, 0:129], qT[:, tok], Dall[:, c - 1, :], True, False, [qkb[0], Dallb], pnb)
                    self.mm(pn[:, 0:129], stm[c % NSTM][:], vp[:, c, :], c == 0, True, [stmb[c % NSTM], vpb], pnb)
                    p.op(p.act, lambda e: e.copy(out=numall[:, c, :], in_=pn[:, 0:129]), reads=[pnb], writes=[numb])
                hs = numall[:, :, 0:128]
                sm, smb = self.small.get()
                p.op(p.act, lambda e: e.activation(out=sm[:, 0:16], in_=numall[:, :, 128], func=AF.Abs), reads=[numb], writes=[smb])
                p.op(p.dve, lambda e: e.tensor_tensor(out=sm[:, 0:16], in0=sm[:, 0:16], in1=clv[:, :, h], op=ALU.max), reads=[smb, gb], writes=[smb])
                p.op(p.dve, lambda e: e.reciprocal(out=sm[:, 0:16], in_=sm[:, 0:16]), reads=[smb], writes=[smb])
                p.op(p.dve, lambda e: e.tensor_tensor(out=hs, in0=hs, in1=sm[:, 0:16].unsqueeze(2).broadcast_to([128, NT, 128]), op=ALU.mult),
                     reads=[numb, smb], writes=[numb])
                sm2, sm2b = self.small.get()
                p.op(p.dve, lambda e: e.tensor_reduce(out=sm2[:, 0:16], in_=hs, axis=AX.X, op=ALU.add), reads=[numb], writes=[sm2b])
                p.op(p.dve, lambda e: e.tensor_scalar(out=sm2[:, 0:16], in0=sm2[:, 0:16], scalar1=1.0 / 128, scalar2=None, op0=ALU.mult), reads=[sm2b], writes=[sm2b])
                p.op(p.dve, lambda e: e.tensor_tensor(out=hs, in0=hs, in1=sm2[:, 0:16].unsqueeze(2).broadcast_to([128, NT, 128]), op=ALU.subtract),
                     reads=[numb, sm2b], writes=[numb])
                sm3, sm3b = self.small.get()
                for c4 in range(4):
                    p.op(p.act, lambda e: e.activation(out=sq4[c4 % 2][:], in_=numall[:, c4 * 4:(c4 + 1) * 4, 0:128], func=AF.Square), reads=[numb], writes=[sq4b[c4 % 2]])
                    p.op(p.dve, lambda e: e.tensor_reduce(out=sm3[:, c4 * 4:(c4 + 1) * 4], in_=sq4[c4 % 2][:], axis=AX.X, op=ALU.add), reads=[sq4b[c4 % 2]], writes=[sm3b])
                p.op(p.act, lambda e: e.activation(out=sm3[:, 0:16], in_=sm3[:, 0:16], func=AF.Sqrt, bias=self.epsc, scale=1.0 / 128), reads=[sm3b, self.cstb], writes=[sm3b])
                p.op(p.dve, lambda e: e.reciprocal(out=sm3[:, 0:16], in_=sm3[:, 0:16]), reads=[sm3b], writes=[sm3b])
                p.op(p.dve, lambda e: e.tensor_tensor(out=hs, in0=hs, in1=sm3[:, 0:16].unsqueeze(2).broadcast_to([128, NT, 128]), op=ALU.mult),
                     reads=[numb, sm3b], writes=[numb])
            def stage_b3(h):
                qT, kT, qkb, vp, vpb, kt, ktb, numall, numb, Dall, Dallb = unpack(h)
                hslab, hslb = hslabs[h]
                slab, slb = hslab[:, :, 384:512], hslb
                for t4 in range(4):
                    pg_, pgb_ = self.ps.get()
                    for k in range(8):
                        self.mm(pg_[:], slab[:, k, :], self.hbuf[:, k, t4 * 512:(t4 + 1) * 512], k == 0, k == 7, [slb, self.hb[t4]], pgb_)
                    p.op(p.act, lambda e: e.activation(out=sgo[t4 % 2][:], in_=pg_[:], func=AF.Sigmoid), reads=[pgb_], writes=[sgob[t4 % 2]])
                    pt, pb = self.ps.get()
                    for j in range(4):
                        t = t4 * 4 + j
                        p.op(p.pe, lambda e: e.transpose(out=pt[:, j * 128:(j + 1) * 128], in_=numall[:, t, 0:128], identity=self.ident),
                             reads=[numb, self.cstb], writes=[pb], pe_accum=(j > 0))
                    p.op(p.dve, lambda e: e.scalar_tensor_tensor(out=oT[:, h, t4 * 512:(t4 + 1) * 512], in0=pt[:], scalar=self.mlp[:, l, 48 + h:49 + h],
                                                                 in1=sgo[t4 % 2][:], op0=ALU.mult, op1=ALU.mult),
                         reads=[pb, sgob[t4 % 2], self.cstb], writes=[ob[t4]])

            for st_name, h in (("a", 0), ("b1", 0), ("a", 1), ("b2", 0), ("b1", 1), ("a", 2), ("b3", 0), ("b2", 1), ("b1", 2), ("a", 3),
                               ("b3", 1), ("b2", 2), ("b1", 3), ("b3", 2), ("b2", 3), ("b3", 3)):
                {"a": stage_a, "b1": stage_b1, "b2": stage_b2, "b3": stage_b3}[st_name](h)
            if "c" in self.mixers:
                self.prefetch(("ng", b, l), wv[:, :, C_NG:C_NG + 24])
                self.prefetch(("nkv0", b, l), wv[:, :, C_NKC:C_NKC + 384])

    def nsa(self, b, l, oT, ob):
        p = self.p
        wv = self.w_in[l].rearrange("(k p) c -> p k c", p=128)
        with self.scope() as sc:
            cmask = self.sb(sc, "ncmask", [128, S], BF16)
            eexp = None
            dmask = self.sb(sc, "ndmask", [128, 256], BF16)
            ka = self.sb(sc, "nka", [128, 16, 64], F32)
            gtm = self.sb(sc, "ngtm", [128, NT, 24], F32)
            cb = Buf("nsac")
            p.dma(p.pool, cmask[:], self.nsa_cmask, writes=[cb])
            p.dma(p.pool, dmask[:], self.nsa_dmask, writes=[cb])
            p.dma(p.sp, ka[:], self.nsa_ka, writes=[cb])
            slab, slb = self.wslab(wv[:, :, C_NG:C_NG + 24], key=("ng", b, l))
            pg, pgb = self.ps.get()
            for t in range(NT):
                for k in range(8):
                    self.mm(pg[:, t * 24:(t + 1) * 24], self.hbuf[:, k, t * 128:(t + 1) * 128], slab[:, k, 0:24], k == 0, k == 7,
                            [slb, self.hb[t // 4]], pgb)
            p.op(p.act, lambda e: e.activation(out=gtm[:].rearrange("p t c -> p (t c)"), in_=pg[:, 0:NT * 24], func=AF.Sigmoid), reads=[pgb], writes=[cb])
            cache = {"_scope": sc}
            for g in range(2):
                self.nsa_group(b, l, g, oT, ob, cache, wv, cmask, eexp, dmask, ka, gtm, cb)
            mi0 = [i for i, m in enumerate("abc") if m in self.mixers][0]
            self.prefetch(("mg_gs", b, l), wv[:, :, (C_GA, C_GB, C_GC)[mi0]:(C_GA, C_GB, C_GC)[mi0] + 512])
            self.prefetch(("mg_up", b, l), self.w_up[mi0][l].rearrange("(k p) n -> p k n", p=128)[:, :, 0:512])

    def nsa_group(self, b, l, g, oT, ob, sg, wv, cmask, eexp, dmask, ka, gtm, cb):
        p = self.p
        cache = sg

        def A(name, shape, dt):
            if name not in cache:
                cache[name] = (self.sb(cache["_scope"], name, shape, dt), Buf(name))
            return cache[name]
        qa, qab = zip(*[A("nqa%d" % i, [128, S], BF16) for i in range(4)])
        kaug, kab = zip(*[A("nka%d" % i, [128, S], BF16) for i in range(2)])
        kcaug, kcab = A("nkca", [128, 128], BF16)
        vsw, vswb = A("nvsw", [128, NT, 2, 65], BF16)
        vcp, vcpb = A("nvcp", [127, 97], BF16)
        oacc, oaccb = A("noacc", [128, 4, 4, 64], F32)
        otmp, otmpb = A("notmp", [128, 4, 64], F32)
        impacc, impb = A("nimp", [128, 4, 32], F32)
        imptmp, imptb = A("nimpt", [128, 4, 32], F32)
        mskb, mskbb = A("nmskb", [128, 4, 32], BF16)
        mbT, mbTb = A("nmbT", [32, 512], BF16)
        LOOKAHEAD, NPT = 3, 5
        PT, PTb = zip(*[A("nPT%d" % i, [128, 512], BF16) for i in range(NPT)])
        pti = [0]

        for w in range(2):
            p.op(p.dve, lambda e: e.memset(kaug[w][64:128, :], 0.0), writes=[kab[w]])
            p.dma(p.pool, kaug[w][96:100, :], self.nsa_krows, writes=[kab[w]])
        p.dma(p.pool, kaug[0][64:96, :], self.nsa_onehot, writes=[kab[0]])
        p.op(p.dve, lambda e: e.memset(kcaug[:], 0.0), writes=[kcab])
        p.dma(p.pool, kcaug[96:100, 0:127], self.nsa_crows, writes=[kcab])
        for r in range(4):
            p.op(p.dve, lambda e: e.memset(qa[r][64:128, :], 0.0), writes=[qab[r]])
            p.dma(p.pool, qa[r][96:100, :], self.nsa_qrows[g * 4 + r], writes=[qab[r]])
        p.dma(p.pool, vcp[:, 64:97], self.nsa_ovl, writes=[vcpb])
        p.op(p.dve, lambda e: e.memset(vsw[:, :, :, 64:65], 1.0), writes=[vswb])
        gbase = C_NKC + g * 384
        slab, slb = self.wslab(wv[:, :, gbase:gbase + 384], key=("nkv%d" % g, b, l))
        kvslab, kvslb = slab, slb
        for tg in range(NG):
            pt, pb = self.ps.get()
            for k in range(8):
                self.mm(pt[:], slab[:, k, 0:128], self.hbuf[:, k, tg * 512:(tg + 1) * 512], k == 0, k == 7, [slb, self.hb[tg]], pb)
            p.op(p.act, lambda e: e.copy(out=kaug[0][0:64, tg * 512:(tg + 1) * 512], in_=pt[0:64, :]), reads=[pb], writes=[kab[0]])
            p.op(p.dve, lambda e: e.tensor_copy(out=kaug[1][0:64, tg * 512:(tg + 1) * 512], in_=pt[64:128, :]), reads=[pb], writes=[kab[1]])
        for t4 in range(4):
            pt, pb = self.ps.get()
            for j in range(4):
                t = t4 * 4 + j
                for k in range(8):
                    self.mm(pt[:, j * 128:(j + 1) * 128], self.hbuf[:, k, t * 128:(t + 1) * 128], slab[:, k, 256:384],
                            k == 0, k == 7, [slb, self.hb[t4]], pb)
            p.op(p.dve, lambda e: e.tensor_copy(out=vsw[:, t4 * 4:(t4 + 1) * 4, :, 0:64], in_=pt[:].rearrange("p (j w c) -> p j w c", j=4, w=2)),
                 reads=[pb], writes=[vswb])
        slab, slb = self.wslab(wv[:, :, C_NQ + g * 256:C_NQ + (g + 1) * 256])
        for rp in range(2):
            for tg in range(NG):
                pt, pb = self.ps.get()
                for k in range(8):
                    self.mm(pt[:], slab[:, k, rp * 128:(rp + 1) * 128], self.hbuf[:, k, tg * 512:(tg + 1) * 512], k == 0, k == 7, [slb, self.hb[tg]], pb)
                p.op(p.act, lambda e: e.activation(out=qa[2 * rp][0:64, tg * 512:(tg + 1) * 512], in_=pt[0:64, :], func=AF.Copy, scale=0.125),
                     reads=[pb], writes=[qab[2 * rp]])
                p.op(p.dve, lambda e: e.tensor_scalar(out=qa[2 * rp + 1][0:64, tg * 512:(tg + 1) * 512], in0=pt[64:128, :], scalar1=0.125, scalar2=None,
                                                      op0=ALU.mult), reads=[pb], writes=[qab[2 * rp + 1]])
        if True:
            kcr, kcrb = A("nkcr", [64, 16, 128], BF16)
            gT, gTb = A("ngT", [64, 128], BF16)
            peT, peb = A("npeT", [64, 2, 32], BF16)
            bia, _biab = A("nbia", [64, 2], F32)
            p.dma(p.pool, peT[:], self.nsa_peT[:, l], writes=[peb])
            slabc, slcb = kvslab, kvslb
            kcr2, kcr2b = A("nkcr2", [64, 16, 128], BF16)
            kcrs = [kcr, kcr2]
            kcrbs = [kcrb, kcr2b]
            for tg in range(NG):
                pt, pb = self.ps.get()
                for k in range(8):
                    self.mm(pt[:], slabc[:, k, 128:256], self.hbuf[:, k, tg * 512:(tg + 1) * 512], k == 0, k == 7, [slcb, self.hb[tg]], pb)
                p.op(p.act, lambda e: e.copy(out=kcrs[0][:, :, tg * 32:(tg + 1) * 32], in_=pt[0:64, :].rearrange("p (m r) -> p r m", r=16)),
                     reads=[pb], writes=[kcrbs[0]])
                p.op(p.dve, lambda e: e.tensor_copy(out=kcrs[1][:, :, tg * 32:(tg + 1) * 32], in_=pt[64:128, :].rearrange("p (m r) -> p r m", r=16)),
                     reads=[pb], writes=[kcrbs[1]])
            for kv in range(2):
                kcr, kcrb = kcrs[kv], kcrbs[kv]
                w1, w1b = self.wslab(self.phi1[kv][l].rearrange("(j d) o -> d j o", d=64))
                w2, w2b = self.wslab(self.phi2[kv][l])
                pp, ppb = self.ps.get()
                for j in range(32):
                    jh, jr = divmod(j, 16)
                    self.mm(pp[0:64, 0:127], w1[:, j, :], kcr[:, jr, jh:jh + 127], j == 0, j == 31, [w1b, kcrb], ppb)
                for j in range(32):
                    self.mm(pp[0:64, 128:129], w1[:, j, :], peT[:, kv, j:j + 1], j == 0, j == 31, [w1b, peb], ppb)
                p.op(p.dve, lambda e: e.tensor_copy(out=bia[:, kv:kv + 1], in_=pp[0:64, 128:129]), reads=[ppb], writes=[gTb])
                p.op(p.act, lambda e: e.activation(out=gT[:, 0:127], in_=pp[0:64, 0:127], func=AF.Gelu_apprx_tanh, bias=bia[:, kv:kv + 1], scale=1.0),
                     reads=[ppb, gTb], writes=[gTb])
                pc, pcb = self.ps.get()
                if kv == 0:
                    self.mm(pc[0:64, 0:127], w2[:, 0:64], gT[:, 0:127], True, True, [w2b, gTb], pcb)
                    p.op(p.act, lambda e: e.copy(out=kcaug[0:64, 0:127], in_=pc[0:64, 0:127]), reads=[pcb], writes=[kcab])
                else:
                    self.mm(pc[0:127, 0:64], gT[:, 0:127], w2[:, 0:64], True, True, [w2b, gTb], pcb)
                    p.op(p.act, lambda e: e.copy(out=vcp[:, 0:64], in_=pc[0:127, 0:64]), reads=[pcb], writes=[vcpb])
        if b == 0 and l == 0 and g == 0:
            self.tap("kcaug", kcaug[:], kcab, [128, 128])
            self.tap("vcp", vcp[:], vcpb, [127, 97])

        def attend(r, qg, ktiles, krhs, kbuf, vfun, nv, first_branch, gcol, want_imp, kdim=128):
            acc, accb = self.ps.get_acc()
            started = [False] * 4
            last_kt = {}
            for (kt, jlo, jhi, masks, nk) in ktiles:
                for j in range(jlo, jhi + 1):
                    last_kt[j] = kt
            def scores(tile):
                (kt, jlo, jhi, masks, nk) = tile
                st_, stb = self.ps.get()
                q0, q1 = qg * 512 + jlo * 128, qg * 512 + (jhi + 1) * 128
                c0, c1 = jlo * 128, (jhi + 1) * 128
                nmm = 1 + len(masks)
                kk = kdim
                self.mm(st_[:, c0:c1], krhs(kt), qa[r][0:kk, q0:q1], True, nmm == 1, [kbuf, qab[r]], stb)
                for mi, (kind, j) in enumerate(masks):
                    lastm = (mi == len(masks) - 1)
                    if kind == "cmp":
                        self.mm(st_[:, c0:c1], self.identb[:], cmask[:, q0:q1], False, lastm, [cb, self.cstb], stb)
                    else:
                        mo = 0 if kind == "causal" else 128
                        self.mm(st_[:, j * 128:(j + 1) * 128], self.identb[:], dmask[:, mo:mo + 128], False, lastm, [cb, self.cstb], stb)
                return st_, stb

            pend, nissued = [], 0
            for ti, (kt, jlo, jhi, masks, nk) in enumerate(ktiles):
                while nissued < len(ktiles) and nissued <= ti + LOOKAHEAD:
                    pend.append(scores(ktiles[nissued]))
                    nissued += 1
                st_, stb = pend.pop(0)
                c0, c1 = jlo * 128, (jhi + 1) * 128
                P_, Pb = PT[pti[0] % NPT], PTb[pti[0] % NPT]
                pti[0] += 1
                p.op(p.act, lambda e: e.activation(out=P_[:, c0:c1], in_=st_[:, c0:c1], func=AF.Exp), reads=[stb], writes=[Pb])
                for j in range(jlo, jhi + 1):
                    self.mm(acc[:, j * nv:(j + 1) * nv], P_[0:nk, j * 128:(j + 1) * 128], vfun(kt), not any(started), last_kt[j] == kt,
                            [Pb, vswb, vcpb], accb, skip=True)
                    started[j] = True
            a3 = acc[:, 0:4 * nv].rearrange("p (j c) -> p j c", c=nv)
            sm, smb = self.small.get()
            p.op(p.dve, lambda e: e.tensor_scalar(out=sm[:, 0:4], in0=a3[:, :, 64], scalar1=1e-30, scalar2=None, op0=ALU.max), reads=[accb], writes=[smb])
            p.op(p.dve, lambda e: e.reciprocal(out=sm[:, 0:4], in_=sm[:, 0:4]), reads=[smb], writes=[smb])
            if want_imp:
                if r == 0:
                    p.op(p.dve, lambda e: e.tensor_tensor(out=impacc[:], in0=a3[:, :, 65:97], in1=sm[:, 0:4].unsqueeze(2).broadcast_to([128, 4, 32]), op=ALU.mult),
                         reads=[accb, smb], writes=[impb])
                else:
                    p.op(p.dve, lambda e: e.tensor_tensor(out=imptmp[:], in0=a3[:, :, 65:97], in1=sm[:, 0:4].unsqueeze(2).broadcast_to([128, 4, 32]), op=ALU.mult),
                         reads=[accb, smb], writes=[imptb])
                    p.op(p.dve, lambda e: e.tensor_tensor(out=impacc[:], in0=impacc[:], in1=imptmp[:], op=ALU.add), reads=[imptb, impb], writes=[impb])
            p.op(p.dve, lambda e: e.tensor_tensor(out=sm[:, 4:8], in0=sm[:, 0:4], in1=gtm[:, qg * 4:(qg + 1) * 4, gcol], op=ALU.mult), reads=[smb, cb], writes=[smb])
            if first_branch:
                p.op(p.dve, lambda e: e.tensor_tensor(out=oacc[:, :, r, :], in0=a3[:, :, 0:64], in1=sm[:, 4:8].unsqueeze(2).broadcast_to([128, 4, 64]), op=ALU.mult),
                     reads=[accb, smb], writes=[oaccb])
            else:
                p.op(p.dve, lambda e: e.tensor_tensor(out=otmp[:], in0=a3[:, :, 0:64], in1=sm[:, 4:8].unsqueeze(2).broadcast_to([128, 4, 64]), op=ALU.mult),
                     reads=[accb, smb], writes=[otmpb])
                p.op(p.dve, lambda e: e.tensor_tensor(out=oacc[:, :, r, :], in0=oacc[:, :, r, :], in1=otmp[:], op=ALU.add), reads=[otmpb, oaccb], writes=[oaccb])

        for qg in range(NG):
            for r in range(4):
                attend(r, qg, [(0, 0, 3, [("cmp", 0)], 127)], lambda kt: kcaug[:, 0:128], kcab, lambda kt: vcp[:, :], 97, True, g * 12 + r * 3 + 0, True)
            p.op(p.dve, lambda e: e.tensor_tensor(out=impacc[:], in0=impacc[:], in1=ka[:, qg * 4:(qg + 1) * 4, 0:32], op=ALU.mult), reads=[impb, cb], writes=[impb])
            p.op(p.dve, lambda e: e.tensor_tensor(out=impacc[:], in0=impacc[:], in1=ka[:, qg * 4:(qg + 1) * 4, 32:64], op=ALU.add), reads=[impb, cb], writes=[impb])
            for j in range(4):
                m8, m8b = self.small.get()
                p.op(p.dve, lambda e: e.max(out=m8[:, 0:8], in_=impacc[:, j, :]), reads=[impb], writes=[m8b])
                p.op(p.dve, lambda e: e.tensor_scalar(out=mskb[:, j, :], in0=impacc[:, j, :], scalar1=m8[:, 7:8], scalar2=1.0, op0=ALU.is_ge, op1=ALU.subtract),
                     reads=[impb, m8b], writes=[mskbb])
            for r in range(4):
                kts = []
                for kt in range(max(0, 4 * qg - 4), 4 * qg + 4):
                    jlo = max(kt - 4 * qg, 0)
                    jhi = min(kt + 4 - 4 * qg, 3)
                    masks = []
                    if kt - 4 * qg >= 0:
                        masks.append(("causal", kt - 4 * qg))
                    if 0 <= kt + 4 - 4 * qg <= 3:
                        masks.append(("band", kt + 4 - 4 * qg))
                    kts.append((kt, jlo, jhi, masks, 128))
                attend(r, qg, kts, lambda kt: kaug[1][:, kt * 128:(kt + 1) * 128], kab[1], lambda kt: vsw[:, kt, 1, :], 65, False, g * 12 + r * 3 + 2, False)
            pm, pmb = self.ps.get()
            pmv = pm[:].bitcast(BF16)
            for j in range(4):
                p.op(p.pe, lambda e: e.transpose(out=pmv[0:32, j * 128:(j + 1) * 128], in_=mskb[:, j, :], identity=self.identb[:]),
                     reads=[mskbb, self.cstb], writes=[pmb], pe_accum=(j > 0))
            p.op(p.act, lambda e: e.activation(out=mbT[:], in_=pmv[0:32, 0:512], func=AF.Copy, scale=30000.0), reads=[pmb], writes=[mbTb])
            for r in range(4):
                p.dma(p.sp, qa[r][64:96, qg * 512:(qg + 1) * 512], mbT[:], reads=[mbTb], writes=[qab[r]])
            for r in range(4):
                kts = []
                for kt in range(4 * qg + 4):
                    i = kt - 4 * qg
                    if i < 0:
                        kts.append((kt, 0, 3, [], 128))
                    else:
                        kts.append((kt, i, 3, [("causal", i)], 128))
                attend(r, qg, kts, lambda kt: kaug[0][:, kt * 128:(kt + 1) * 128], kab[0], lambda kt: vsw[:, kt, 0, :], 65, False, g * 12 + r * 3 + 1, False)
            for rp in range(2):
                pt, pb = self.ps.get()
                for j in range(4):
                    p.op(p.pe, lambda e: e.transpose(out=pt[:, j * 128:(j + 1) * 128], in_=oacc[:, j, rp * 2:rp * 2 + 2, :].rearrange("p r d -> p (r d)"),
                                                     identity=self.ident), reads=[oaccb, self.cstb], writes=[pb], pe_accum=(j > 0))
                p.op(p.act, lambda e: e.copy(out=oT[:, g * 2 + rp, qg * 512:(qg + 1) * 512], in_=pt[:]), reads=[pb], writes=[ob[qg]])


CO_ID = 0
CO_EPS = 128
CO_TRI = 129
CO_ONE = 257
CO_LNS = 258
CO_ONES = 259
NCONST = 259 + 128


def make_consts():
    c = np.zeros((128, NCONST), np.float32)
    c[:, CO_ID:CO_ID + 128] = np.eye(128, dtype=np.float32)
    c[:, CO_EPS] = EPS
    pp = np.arange(128)
    c[:, CO_TRI:CO_TRI + 128] = (pp[:, None] <= pp[None, :]).astype(np.float32)
    c[:, CO_ONE] = 1.0
    c[:, CO_LNS] = -0.5 * np.log(128.0)
    c[:, CO_ONES:CO_ONES + 128] = 1.0
    return c


def nsa_consts():
    d = {}
    pos = np.arange(S)
    d["nsa_krows"] = np.stack([64.0 * (pos // 64), 1.0 * (pos % 64), np.ones(S), np.ones(S)]).astype(np.float32)
    slopes = 2.0 ** (-(np.arange(8) + 1.0))
    R = np.stack([np.ones(S), np.ones(S), -64.0 * (pos // 64), -1.0 * (pos % 64)])
    d["nsa_qrows"] = (slopes[:, None, None] * R[None]).astype(np.float32)
    cc = np.arange(127)
    d["nsa_crows"] = np.stack([16.0 * cc, np.full(127, 15.5), np.ones(127), np.ones(127)]).astype(np.float32)
    cend = 16 * cc + 31
    cm = np.zeros((128, S), np.float32)
    cm[0:127] = np.where(cend[:, None] <= pos[None, :], 0.0, -30000.0)
    d["nsa_cmask"] = cm
    d["nsa_onehot"] = (np.arange(32)[:, None] == (pos[None, :] // 64)).astype(np.float32)
    pp = np.arange(128)
    dm = np.zeros((128, 256), np.float32)
    dm[:, 0:128] = np.where(pp[:, None] <= pp[None, :], 0.0, -30000.0)
    dm[:, 128:256] = np.where(pp[:, None] > pp[None, :], 0.0, -30000.0)
    d["nsa_dmask"] = dm
    t = pos.reshape(16, 128).T
    jt = t // 64
    jj = np.arange(32)[None, None, :]
    fut = jj > jt[:, :, None]
    forced = ((jj == 0) | (jj == jt[:, :, None]) | (jj == jt[:, :, None] - 1)) & ~fut
    keep = np.where(fut | forced, 0.0, 1.0)
    add = np.where(fut, -1e30, np.where(forced, 1e4, 0.0))
    d["nsa_ka"] = np.concatenate([keep, add], axis=2).astype(np.float32)
    cs = 16 * cc
    sel = np.arange(32)
    ovl = ((cs[:, None] <= sel[None, :] * 64 + 63) & (cs[:, None] + 31 >= sel[None, :] * 64)).astype(np.float32)
    d["nsa_ovl"] = np.concatenate([np.ones((127, 1), np.float32), ovl], axis=1)
    return d


def prep_shared(inp, L):
    f = lambda a: np.ascontiguousarray(np.asarray(a, dtype=np.float32))
    d = {}
    d["w_ada"] = f(inp["w_ada"][:L])
    d["b_adaT"] = f(np.asarray(inp["b_ada"])[:L].reshape(L, 48, 128).transpose(2, 0, 1))
    perm = np.arange(PT)
    for g in range(2):
        for i, c0 in enumerate((C_NKS, C_NKW, C_NKC, C_NVC, C_NVS, C_NVW)):
            perm[C_NKC + g * 384 + i * 64: C_NKC + g * 384 + (i + 1) * 64] = np.arange(c0 + g * 64, c0 + (g + 1) * 64)
    for h in range(4):
        for i, c0 in enumerate((C_MQ, C_MK, C_MV, C_MO)):
            perm[C_MQ + h * 512 + i * 128: C_MQ + h * 512 + (i + 1) * 128] = np.arange(c0 + h * 128, c0 + (h + 1) * 128)
    d["w_in"] = f(np.asarray(inp["w_in"], dtype=np.float32)[:L][:, :, perm])
    gn = []
    for l in range(L):
        gn.append(np.asarray(inp["g_norm1"])[l])
        gn.append(np.asarray(inp["g_norm2"])[l])
    gn.append(np.asarray(inp["g_final"]))
    d["gnT"] = f(np.stack(gn).reshape(2 * L + 1, 8, 128).transpose(2, 0, 1))
    for m in "abc":
        d["w_up_" + m] = f(inp["w_up_" + m][:L])
    d["w_out"] = f(inp["w_out"][:L])
    d["w_mlp1"] = f(inp["w_mlp1"][:L])
    d["w_mlp2"] = f(inp["w_mlp2"][:L])
    d["consts"] = make_consts()
    A = lambda k: np.asarray(inp[k], dtype=np.float32)[:L]
    ln = np.stack([A("gm_ln_g"), A("gm_ln_b")], axis=1)
    d["gm_ln"] = f(np.broadcast_to(ln[None], (128, L, 2, 512)))
    d["gm_wsT"] = f(A("gm_ws").transpose(3, 0, 1, 2))
    d["gm_bs"] = f(A("gm_bs").reshape(1, L * 4 * 128))
    mlp = np.zeros((128, L, 52), np.float32)
    mlp[:, :, 0:32] = A("ml_conv_w").reshape(L, 4, 8, 128).transpose(3, 0, 2, 1).reshape(128, L, 32)
    mlp[:, :, 32:40] = A("ml_conv_b").reshape(L, 8, 128).transpose(2, 0, 1)
    mlp[:, :, 40:48] = A("ml_gate_b")[None]
    mlp[:, :, 48:52] = A("ml_norm_g").reshape(L, 4, 128).transpose(2, 0, 1)
    d["ml_par"] = mlp
    d["nsa_peT"] = f(np.stack([A("nsa_pe_k"), A("nsa_pe_v")], axis=1).transpose(3, 0, 1, 2))
    for nm in ("nsa_phi_k1", "nsa_phi_v1", "nsa_phi_k2", "nsa_phi_v2"):
        d[nm] = f(A(nm))
    d.update(nsa_consts())
    return d


def run(inp, ncores, nseq, L, mixers="abc", taps=()):
    global LAST_RES
    k = K(nseq, L, mixers, taps)
    nc = k.build()
    shared = prep_shared(inp, L)
    x = np.asarray(inp["x"], dtype=np.float32)
    c = np.asarray(inp["c"], dtype=np.float32)
    in_maps = []
    for i in range(ncores):
        m = dict(shared)
        m["x"] = np.ascontiguousarray(x[i * nseq:(i + 1) * nseq].reshape(nseq * S, D))
        m["cT"] = np.ascontiguousarray(c[i * nseq:(i + 1) * nseq].reshape(nseq, 8, 128).transpose(2, 1, 0))
        in_maps.append(m)
    res = run_bass_kernel_spmd(nc, in_maps, core_ids=list(range(ncores)))
    out = np.concatenate([r["out"].reshape(nseq, S, D) for r in res.results], axis=0)
    return out, res, k


def kernel(**inputs):
    out, _, _ = run(inputs, 8, 4, 2)
    return out.astype(np.float32)
```
